# Optimizing a Trainium2 kernel written in Bass

```python
import math
import jax, jax.numpy as jnp
from jax import lax
import numpy as np


D_MODEL = 2048
BATCH = 8
SEQ = 2048
DEPTH = 1

GRID_W = 64
CTX_LEN = 256
N_MOD = 9
D_FF = ((8 * D_MODEL // 3 + 255) // 256) * 256
DN_HEAD_DIM = 128
DN_HEADS = (3 * D_MODEL // 4) // DN_HEAD_DIM
DN_QK = DN_HEADS * DN_HEAD_DIM
DN_V = DN_HEADS * DN_HEAD_DIM
DN_CONV = 2 * DN_QK + DN_V
DN_COLS = DN_CONV + DN_V + 4 * DN_HEADS
CHUNK = 64
CONV_K = 3
S5_WIDTH = D_MODEL - DN_V
S5_GROUP = 16
S5_GROUPS = S5_WIDTH // S5_GROUP
S5_STATE = 64
IN_COLS = DN_COLS + S5_WIDTH
EPS = 1e-6

kernel_name = 'hybrid_deltanet_s5_macaron_dit_block'


def rmsnorm(x, w):
    xf = x.astype(jnp.float32)
    y = xf * lax.rsqrt(jnp.mean(xf * xf, axis=-1, keepdims=True) + EPS)
    return (y * w.astype(jnp.float32)).astype(x.dtype)


def ada_params(cond, w_mod, b_mod):
    m = jax.nn.silu(cond) @ w_mod + b_mod
    return m.reshape(cond.shape[:-1] + (N_MOD, -1))


def modulated_norm(h, norm_w, m, j):
    return rmsnorm(h, norm_w) * (1.0 + m[:, 3 * j + 1, None]) + m[:, 3 * j, None]


def swiglu(h, w_up, w_down):
    g, u = jnp.split(h @ w_up, 2, axis=-1)
    return (jax.nn.silu(g) * u) @ w_down


def orient(t, d):
    return t if d == 0 else jnp.flip(t, axis=1)


def short_conv(t, w, rows):
    b, l, ch = t.shape
    y = lax.conv_general_dilated(
        t.astype(jnp.float32).reshape(b, rows, l // rows, ch),
        w.astype(jnp.float32)[:, :, None, :],
        window_strides=(1, 1), padding='SAME',
        dimension_numbers=('NHWC', 'HWIO', 'NHWC'), feature_group_count=ch)
    return jax.nn.silu(y.reshape(b, l, ch))


def l2norm(t):
    return t * lax.rsqrt(jnp.sum(t * t, axis=-1, keepdims=True) + EPS)


def to_chunks(t, n):
    b, l, h = t.shape[:3]
    t = t.reshape((b, n, CHUNK, h) + t.shape[3:])
    return jnp.moveaxis(t, (1, 3), (0, 2))


def delta_rule_chunked(q, k, v, g, beta, s0, with_output):
    b, l, h, dk = q.shape
    n = l // CHUNK
    qc, kc, vc = to_chunks(q, n), to_chunks(k, n), to_chunks(v, n)
    bc = to_chunks(beta, n)
    gcum = jnp.cumsum(to_chunks(g, n), axis=-1)
    idx = jnp.arange(CHUNK)
    lower_incl = idx[:, None] >= idx[None, :]
    decay = jnp.exp(jnp.where(lower_incl, gcum[..., :, None] - gcum[..., None, :], -jnp.inf))
    a_mat = jnp.where(idx[:, None] > idx[None, :],
                      bc[..., :, None] * decay * jnp.einsum('nbhid,nbhjd->nbhij', kc, kc), 0.0)
    rhs = jnp.concatenate([(bc * jnp.exp(gcum))[..., None] * kc, bc[..., None] * vc], axis=-1)
    sol = lax.linalg.triangular_solve(a_mat + jnp.eye(CHUNK, dtype=jnp.float32), rhs,
                                      left_side=True, lower=True, unit_diagonal=True)
    w, u = sol[..., :dk], sol[..., dk:]
    k_dec = kc * jnp.exp(gcum[..., -1:] - gcum)[..., None]
    g_last = jnp.exp(gcum[..., -1])
    if with_output:
        qk = jnp.einsum('nbhid,nbhjd->nbhij', qc, kc) * decay
        q_dec = qc * jnp.exp(gcum)[..., None]
        xs = (w, u, k_dec, g_last, qk, q_dec)
    else:
        xs = (w, u, k_dec, g_last)

    def step(s, xs_i):
        u_i = xs_i[1] - jnp.einsum('bhck,bhkv->bhcv', xs_i[0], s)
        s_new = xs_i[3][..., None, None] * s + jnp.einsum('bhck,bhcv->bhkv', xs_i[2], u_i)
        if with_output:
            o = (jnp.einsum('bhck,bhkv->bhcv', xs_i[5], s)
                 + jnp.einsum('bhij,bhjv->bhiv', xs_i[4], u_i))
            return s_new, o
        return s_new, None

    s_fin, o = lax.scan(step, s0, xs)
    if not with_output:
        return None, s_fin
    o = jnp.moveaxis(o, (0, 2), (1, 3)).reshape(b, l, h, -1)
    return o, s_fin


def dn_streams(p, conv_w, a_log, dt_bias, rows):
    b, l, _ = p.shape
    qkv = short_conv(p[..., :DN_CONV], conv_w, rows)
    q = l2norm(qkv[..., :DN_QK].reshape(b, l, DN_HEADS, DN_HEAD_DIM)) * DN_HEAD_DIM ** -0.5
    k = l2norm(qkv[..., DN_QK:2 * DN_QK].reshape(b, l, DN_HEADS, DN_HEAD_DIM))
    v = qkv[..., 2 * DN_QK:].reshape(b, l, DN_HEADS, DN_HEAD_DIM)
    gb = p[..., DN_CONV + DN_V:].astype(jnp.float32).reshape(b, l, 4, DN_HEADS)
    g = -jnp.exp(a_log.astype(jnp.float32)) * jax.nn.softplus(gb[:, :, :2] + dt_bias.astype(jnp.float32))
    beta = jax.nn.sigmoid(gb[:, :, 2:])
    return q, k, v, g, beta


def run_direction(streams, d, s0, with_output):
    q, k, v, g, beta = streams
    return delta_rule_chunked(orient(q, d), orient(k, d), orient(v, d),
                              orient(g[:, :, d], d), orient(beta[:, :, d], d), s0, with_output)


def gated_head_norm(o, z, w):
    b, l = o.shape[:2]
    y = o * lax.rsqrt(jnp.mean(o * o, axis=-1, keepdims=True) + EPS) * w.astype(jnp.float32)
    y = y * jax.nn.silu(z.astype(jnp.float32).reshape(b, l, DN_HEADS, DN_HEAD_DIM))
    return y.reshape(b, l, DN_V).astype(z.dtype)


def gated_deltanet(p_lat, p_ctx, rows, conv_w, a_log, dt_bias, norm_w, ctx_out):
    lat = dn_streams(p_lat, conv_w, a_log, dt_bias, rows)
    cx = dn_streams(p_ctx, conv_w, a_log, dt_bias, 1)
    b = p_lat.shape[0]
    o_lat, o_ctx = 0.0, 0.0
    for d in range(2):
        s0 = jnp.zeros((b, DN_HEADS, DN_HEAD_DIM, DN_HEAD_DIM), jnp.float32)
        oc, s_ctx = run_direction(cx, d, s0, ctx_out)
        ol, _ = run_direction(lat, d, s_ctx, True)
        o_lat = o_lat + orient(ol, d)
        if ctx_out:
            o_ctx = o_ctx + orient(oc, d)
    out_lat = gated_head_norm(o_lat, p_lat[..., DN_CONV:DN_CONV + DN_V], norm_w)
    out_ctx = gated_head_norm(o_ctx, p_ctx[..., DN_CONV:DN_CONV + DN_V], norm_w) if ctx_out else None
    return out_lat, out_ctx


def ssm_combine(e1, e2):
    a1, b1 = e1
    a2, b2 = e2
    return a1 * a2, a2 * b1 + b2


def s5_discretise(a_re, a_im, log_dt, b_mat):
    lam = lax.complex(a_re.astype(jnp.float32), a_im.astype(jnp.float32))
    a_bar = jnp.exp(lam * jnp.exp(log_dt.astype(jnp.float32))[:, None])
    b_bar = ((a_bar - 1.0) / lam)[..., None] * b_mat
    return a_bar, b_bar


def s5_scan(u, a_bar, b_bar, h0):
    bu = jnp.einsum('gps,blgs->blgp', b_bar, u.astype(jnp.complex64))
    if h0 is not None:
        bu = bu.at[:, 0].add(a_bar * h0)
    a = jnp.broadcast_to(a_bar, (1, u.shape[1]) + a_bar.shape)
    return lax.associative_scan(ssm_combine, (a, bu), axis=1)[1]


def s5_readout(h, u, c_mat, d_skip, w_glu):
    b, l = u.shape[:2]
    y = (jnp.real(jnp.einsum('gsp,blgp->blgs', c_mat, h))
         + d_skip.astype(jnp.float32).reshape(S5_GROUPS, S5_GROUP) * u)
    y = jax.nn.gelu(y.reshape(b, l, S5_WIDTH))
    ya, yb = jnp.split(y @ w_glu.astype(jnp.float32), 2, axis=-1)
    return ya * jax.nn.sigmoid(yb)


def s5_mixer(u_lat, u_ctx, rows, a_re, a_im, log_dt, b_re, b_im, c_re, c_im, d_skip, w_glu, ctx_out):
    b, l, _ = u_lat.shape
    ul = (u_lat.astype(jnp.float32).reshape(b, rows, GRID_W, S5_WIDTH)
          .transpose(0, 2, 1, 3).reshape(b, l, S5_GROUPS, S5_GROUP))
    uc = u_ctx.astype(jnp.float32).reshape(b, u_ctx.shape[1], S5_GROUPS, S5_GROUP)
    b_mat = lax.complex(b_re.astype(jnp.float32), b_im.astype(jnp.float32))
    c_mat = lax.complex(c_re.astype(jnp.float32), c_im.astype(jnp.float32))
    h_lat, h_ctx = 0.0, 0.0
    for d in range(2):
        a_bar, b_bar = s5_discretise(a_re[d], a_im[d], log_dt[d], b_mat)
        hc = s5_scan(orient(uc, d), a_bar, b_bar, None)
        hl = s5_scan(orient(ul, d), a_bar, b_bar, hc[:, -1])
        h_lat = h_lat + orient(hl, d)
        if ctx_out:
            h_ctx = h_ctx + orient(hc, d)
    y_lat = s5_readout(h_lat, ul, c_mat, d_skip, w_glu)
    y_lat = (y_lat.reshape(b, GRID_W, rows, S5_WIDTH).transpose(0, 2, 1, 3)
             .reshape(b, l, S5_WIDTH).astype(u_lat.dtype))
    y_ctx = s5_readout(h_ctx, uc, c_mat, d_skip, w_glu).astype(u_ctx.dtype) if ctx_out else None
    return y_lat, y_ctx


def trunk_layer(x, ctx, c, c_ctx, update_ctx, w_mod, b_mod, norm_ffn1, ffn1_up, ffn1_down,
                norm_mix, w_in, dn_conv, dn_a_log, dn_dt_bias, dn_norm,
                s5_a_re, s5_a_im, s5_log_dt, s5_b_re, s5_b_im, s5_c_re, s5_c_im, s5_d, s5_glu,
                w_out, norm_ffn2, ffn2_up, ffn2_down):
    rows = x.shape[1] // GRID_W
    m_lat = ada_params(c, w_mod, b_mod)
    m_ctx = ada_params(c_ctx, w_mod, b_mod)[None]
    x = x + 0.5 * m_lat[:, 2, None] * swiglu(modulated_norm(x, norm_ffn1, m_lat, 0), ffn1_up, ffn1_down)
    ctx = ctx + 0.5 * m_ctx[:, 2, None] * swiglu(modulated_norm(ctx, norm_ffn1, m_ctx, 0), ffn1_up, ffn1_down)
    p_lat = modulated_norm(x, norm_mix, m_lat, 1) @ w_in
    p_ctx = modulated_norm(ctx, norm_mix, m_ctx, 1) @ w_in
    dn_lat, dn_ctx = gated_deltanet(p_lat[..., :DN_COLS], p_ctx[..., :DN_COLS], rows,
                                    dn_conv, dn_a_log, dn_dt_bias, dn_norm, update_ctx)
    s5_lat, s5_ctx = s5_mixer(p_lat[..., DN_COLS:], p_ctx[..., DN_COLS:], rows,
                              s5_a_re, s5_a_im, s5_log_dt, s5_b_re, s5_b_im, s5_c_re, s5_c_im,
                              s5_d, s5_glu, update_ctx)
    x = x + m_lat[:, 5, None] * (jnp.concatenate([dn_lat, s5_lat], axis=-1) @ w_out)
    x = x + 0.5 * m_lat[:, 8, None] * swiglu(modulated_norm(x, norm_ffn2, m_lat, 2), ffn2_up, ffn2_down)
    if update_ctx:
        ctx = ctx + m_ctx[:, 5, None] * (jnp.concatenate([dn_ctx, s5_ctx], axis=-1) @ w_out)
        ctx = ctx + 0.5 * m_ctx[:, 8, None] * swiglu(modulated_norm(ctx, norm_ffn2, m_ctx, 2), ffn2_up, ffn2_down)
    return x, ctx


def setup_inputs(seed: int = 0) -> dict:
    key = jax.random.key(seed)
    ks = iter(list(jax.random.split(key, 40)))
    f32 = jnp.float32
    L = DEPTH

    def nrm(shape, scale):
        return jax.random.normal(next(ks), shape, f32) * scale

    def unif(shape, lo, hi):
        return jax.random.uniform(next(ks), shape, f32, lo, hi)

    dt = jnp.exp(unif((L, 2, DN_HEADS), math.log(1e-3), math.log(1e-1)))
    n_idx = jnp.arange(S5_STATE, dtype=f32)
    return {
        'x': nrm((BATCH, SEQ, D_MODEL), 1.0),
        'c': nrm((BATCH, D_MODEL), 1.0),
        'ctx': nrm((BATCH, CTX_LEN, D_MODEL), 1.0),
        'c_ctx': nrm((D_MODEL,), 1.0),
        'w_mod': nrm((L, D_MODEL, N_MOD * D_MODEL), D_MODEL ** -0.5),
        'b_mod': nrm((L, N_MOD * D_MODEL), 0.02),
        'norm_ffn1': 1.0 + nrm((L, D_MODEL), 0.02),
        'ffn1_up': nrm((L, D_MODEL, 2 * D_FF), D_MODEL ** -0.5),
        'ffn1_down': nrm((L, D_FF, D_MODEL), D_FF ** -0.5),
        'norm_mix': 1.0 + nrm((L, D_MODEL), 0.02),
        'w_in': nrm((L, D_MODEL, IN_COLS), D_MODEL ** -0.5),
        'dn_conv': nrm((L, CONV_K, CONV_K, DN_CONV), 1.0 / CONV_K),
        'dn_a_log': jnp.log(unif((L, 2, DN_HEADS), 1.0, 16.0)),
        'dn_dt_bias': dt + jnp.log(-jnp.expm1(-dt)),
        'dn_norm': 1.0 + nrm((L, DN_HEAD_DIM), 0.02),
        's5_a_re': -0.5 + nrm((L, 2, S5_GROUPS, S5_STATE), 0.01),
        's5_a_im': jnp.pi * n_idx + nrm((L, 2, S5_GROUPS, S5_STATE), 0.01),
        's5_log_dt': unif((L, 2, S5_GROUPS), math.log(1e-3), math.log(1e-1)),
        's5_b_re': nrm((L, S5_GROUPS, S5_STATE, S5_GROUP), (2 * S5_GROUP) ** -0.5),
        's5_b_im': nrm((L, S5_GROUPS, S5_STATE, S5_GROUP), (2 * S5_GROUP) ** -0.5),
        's5_c_re': nrm((L, S5_GROUPS, S5_GROUP, S5_STATE), S5_STATE ** -0.5),
        's5_c_im': nrm((L, S5_GROUPS, S5_GROUP, S5_STATE), S5_STATE ** -0.5),
        's5_d': nrm((L, S5_WIDTH), 1.0),
        's5_glu': nrm((L, S5_WIDTH, 2 * S5_WIDTH), S5_WIDTH ** -0.5),
        'w_out': nrm((L, D_MODEL, D_MODEL), D_MODEL ** -0.5),
        'norm_ffn2': 1.0 + nrm((L, D_MODEL), 0.02),
        'ffn2_up': nrm((L, D_MODEL, 2 * D_FF), D_MODEL ** -0.5),
        'ffn2_down': nrm((L, D_FF, D_MODEL), D_FF ** -0.5),
        'final_norm': 1.0 + nrm((D_MODEL,), 0.02),
    }


def reference(x, c, ctx, c_ctx, w_mod, b_mod, norm_ffn1, ffn1_up, ffn1_down, norm_mix, w_in,
              dn_conv, dn_a_log, dn_dt_bias, dn_norm, s5_a_re, s5_a_im, s5_log_dt, s5_b_re, s5_b_im,
              s5_c_re, s5_c_im, s5_d, s5_glu, w_out, norm_ffn2, ffn2_up, ffn2_down, final_norm):
    for i in range(DEPTH):
        x, ctx = trunk_layer(x, ctx, c, c_ctx, i + 1 < DEPTH, w_mod[i], b_mod[i],
                             norm_ffn1[i], ffn1_up[i], ffn1_down[i], norm_mix[i], w_in[i],
                             dn_conv[i], dn_a_log[i], dn_dt_bias[i], dn_norm[i],
                             s5_a_re[i], s5_a_im[i], s5_log_dt[i], s5_b_re[i], s5_b_im[i],
                             s5_c_re[i], s5_c_im[i], s5_d[i], s5_glu[i],
                             w_out[i], norm_ffn2[i], ffn2_up[i], ffn2_down[i])
    return rmsnorm(x, final_norm)
```

```python
import numpy as np
import concourse.bass as bass
import concourse.mybir as mybir
from concourse.bass_utils import run_bass_kernel_spmd

F32 = mybir.dt.float32
BF16 = mybir.dt.bfloat16
F32R = mybir.dt.float32r
AF = mybir.ActivationFunctionType
ALU = mybir.AluOpType
S_ = np.s_

D = 2048
SEQ = 2048
CTX = 256
NT = SEQ + CTX
KC = D // 128
DFF = 5632
NJ = DFF // 128
NH = 12
EPS = 1e-6
NMOD = 9


class Sem:
    def __init__(self, nc, name):
        self.h = nc.alloc_semaphore(name)
        self.count = 0


class Buf:
    __slots__ = ("w", "r", "name", "excl")

    def __init__(self, name=""):
        self.w = {}
        self.r = {}
        self.name = name
        self.excl = False


class View:
    __slots__ = ("ap", "bufs")

    def __init__(self, ap, bufs):
        self.ap = ap
        self.bufs = bufs


class Tile:
    def __init__(self, kb, name, shape, dtype, nbuf=1, space="sbuf"):
        nc = kb.nc
        kb.uid += 1
        name = f"{name}_{kb.uid}"
        if space == "sbuf":
            if kb.stack is not None:
                self.t = kb.stack.enter_context(nc.sbuf_tensor(name, list(shape), dtype))
            else:
                self.t = nc.alloc_sbuf_tensor(name, list(shape), dtype)
        elif space == "psum":
            self.t = nc.alloc_psum_tensor(name, list(shape), dtype)
        self.bufs = [Buf(f"{name}.{i}") for i in range(nbuf)]
        if space == "psum":
            for b in self.bufs:
                b.excl = True
        self.shape = shape

    def v(self, idx=None, bufs=None):
        ap = self.t[idx] if idx is not None else self.t[:]
        if bufs is None:
            bl = self.bufs
        elif isinstance(bufs, int):
            bl = [self.bufs[bufs]]
        else:
            bl = [self.bufs[i] for i in bufs]
        return View(ap, bl)


class Dram:
    def __init__(self, kb, name, shape, dtype, kind="Internal", nbuf=1):
        self.t = kb.nc.dram_tensor(name, list(shape), dtype, kind=kind)
        self.ap = self.t.ap()
        self.bufs = [Buf(f"{name}.{i}") for i in range(nbuf)]

    def v(self, ap, bufs=None):
        if bufs is None:
            bl = self.bufs
        elif isinstance(bufs, int):
            bl = [self.bufs[bufs]]
        else:
            bl = [self.bufs[i] for i in bufs]
        return View(ap, bl)


class Eng:
    def __init__(self, nc, name, obj):
        self.name = name
        self.obj = obj
        self.sem = Sem(nc, "e_" + name)
        self.seen = {}


class KB:
    NDMA = 24

    def __init__(self, nc):
        self.nc = nc
        self.E = {
            "pe": Eng(nc, "pe", nc.tensor),
            "dve": Eng(nc, "dve", nc.vector),
            "act": Eng(nc, "act", nc.scalar),
            "pool": Eng(nc, "pool", nc.gpsimd),
            "sp": Eng(nc, "sp", nc.sync),
        }
        self.dsem_q = {"sp": [Sem(nc, f"d{i}") for i in range(self.NDMA)],
                       "pool": [Sem(nc, f"dsw{i}") for i in range(12)]}
        self.dsem_q["act"] = self.dsem_q["sp"]
        self.dsem = self.dsem_q["sp"] + self.dsem_q["pool"]
        self.dnext_q = {"sp": 0, "pool": 0}
        self.nins = 0
        self.stack = None
        self.uid = 0
        self.psum = None
        self.pbi = 0

    def phase(self):
        return Phase(self)

    def bank(self):
        b = self.psum[self.pbi % 8]
        self.pbi += 1
        return b

    def barrier(self):
        sems = [e.sem for e in self.E.values()] + self.dsem
        for E in self.E.values():
            for s in sems:
                if s is E.sem or s.count == 0:
                    continue
                if E.seen.get(s, 0) < s.count:
                    E.obj.wait_ge(s.h, s.count)
                    E.seen[s] = s.count
                    self.nins += 1

    def _need(self, E, reads, writes, adds):
        need = {}
        own = E.sem

        def req(s, v):
            if need.get(s, 0) < v:
                need[s] = v

        for b in reads:
            for s, v in b.w.items():
                req(s, v)
            if b.excl:
                for s, v in b.r.items():
                    if s is not own:
                        req(s, v)
        for b in writes:
            for s, v in b.w.items():
                if s is not own:
                    req(s, v)
            for s, v in b.r.items():
                if s is not own:
                    req(s, v)
        for b in adds:
            for s, v in b.r.items():
                if s is not own:
                    req(s, v)
        if E.name == "pe":
            need.pop(own, None)
        return need

    def _wait(self, E, need):
        for s, v in need.items():
            if E.seen.get(s, 0) < v:
                E.obj.wait_ge(s.h, v)
                E.seen[s] = v
                self.nins += 1

    def _mark(self, sem, val, reads, writes, adds):
        for b in reads:
            if b.r.get(sem, 0) < val:
                b.r[sem] = val
        for b in writes:
            b.w = {sem: val}
            b.r = {}
        for b in adds:
            b.w[sem] = val

    def I(self, eng, method, out, *args, adds=False, extra_reads=(), **kw):
        E = self.E[eng]
        reads = []
        for b in extra_reads:
            reads.extend(b.bufs if isinstance(b, View) else [b])
        cargs = []
        for a in args:
            if isinstance(a, View):
                reads.extend(a.bufs)
                cargs.append(a.ap)
            else:
                cargs.append(a)
        ckw = {}
        wl = list(out.bufs)
        for k, a in kw.items():
            if isinstance(a, View):
                if k == "accum_out":
                    wl.extend(a.bufs)
                else:
                    reads.extend(a.bufs)
                ckw[k] = a.ap
            else:
                ckw[k] = a
        writes = [] if adds else wl
        addl = wl if adds else []
        need = self._need(E, reads, writes, addl)
        self._wait(E, need)
        ins = getattr(E.obj, method)(out.ap, *cargs, **ckw)
        E.sem.count += 1
        ins.then_inc(E.sem.h, 1)
        self._mark(E.sem, E.sem.count, reads, writes, addl)
        self.nins += 1
        return ins

    def dma(self, q, out, in_, adds=False):
        E = self.E[q]
        qk = "pool" if q == "pool" else "sp"
        pool_ = self.dsem_q[qk]
        slot = pool_[self.dnext_q[qk]]
        self.dnext_q[qk] = (self.dnext_q[qk] + 1) % len(pool_)
        reads = list(in_.bufs)
        wl = list(out.bufs)
        writes = [] if adds else wl
        addl = wl if adds else []
        need = {}
        own = None

        def req(s, v):
            if need.get(s, 0) < v:
                need[s] = v

        for b in reads:
            for s, v in b.w.items():
                req(s, v)
        for b in writes:
            for s, v in b.w.items():
                req(s, v)
            for s, v in b.r.items():
                req(s, v)
        esems = {e.sem for e in self.E.values()}
        for b in addl:
            for s, v in b.r.items():
                req(s, v)
            for s, v in b.w.items():
                if s in esems:
                    req(s, v)
        if slot.count:
            req(slot, slot.count)
        self._wait(E, need)
        ins = E.obj.dma_start(out=out.ap, in_=in_.ap)
        slot.count += 16
        ins.then_inc(slot.h, 16)
        self._mark(slot, slot.count, reads, writes, addl)
        self.nins += 1
        return ins

    def finish(self, bufs):
        E = self.E["sp"]
        need = {}
        for b in bufs:
            for s, v in b.w.items():
                if need.get(s, 0) < v:
                    need[s] = v
        for s, v in need.items():
            E.obj.wait_ge(s.h, v)


class Phase:
    def __init__(self, kb):
        self.kb = kb

    def __enter__(self):
        from contextlib import ExitStack
        self.prev = self.kb.stack
        self.stack = ExitStack()
        self.kb.stack = self.stack
        return self

    def __exit__(self, et, ev, tb):
        if et is None:
            self.kb.barrier()
            self.stack.close()
        self.kb.stack = self.prev
        return False


def tok_of_tile(tt):
    return tt * 128


def f32v(v):
    return View(v.ap.bitcast(F32), v.bufs)


def delta_pre(kb, w, q, bank, d, h, tt, KQ, ktok, vtok, tabs, cm):
    Gam, eG, bEG, kds, glast, btab = tabs
    ident, Lincl, Uincl, Lstr, Ustr, BD, NBD = cm
    col = d * 12 + h
    gcol = Gam.v(S_[:, tt, col:col + 1])
    bcol = btab.v(S_[:, tt, col:col + 1])
    kTc = KQ.v(S_[:, tt, 0:128])
    qTc = KQ.v(S_[:, tt, 128:256])
    is_lat = tt < 16
    maskS = Lstr if d == 0 else Ustr
    maskI = Lincl if d == 0 else Uincl
    I = kb.I
    pA = bank
    I("pe", "matmul", pA.v(S_[:, 0:128]), View(Gam.t[:, tt, col:col + 1].to_broadcast([128, 128]), Gam.bufs), ident, start=True, stop=True)
    if is_lat:
        I("pe", "matmul", pA.v(S_[:, 128:384]), kTc, KQ.v(S_[:, tt, 0:256]), start=True, stop=True)
    else:
        I("pe", "matmul", pA.v(S_[:, 128:256]), kTc, kTc, start=True, stop=True)
    yield
    I("dve", "tensor_scalar", w.F.v(), pA.v(S_[:, 0:128]), gcol, 0.0, ALU.subtract, ALU.max)
    if is_lat:
        I("dve", "tensor_scalar", w.E2.v(), pA.v(S_[:, 0:128]), gcol, 0.0, ALU.subtract, ALU.min)
    yield
    I("act", "activation", w.F.v(), w.F.v(), AF.Exp, scale=-1.0)
    if is_lat:
        I("act", "activation", w.E2.v(), w.E2.v(), AF.Exp)
        I("act", "activation", w.EB.v(), pA.v(S_[:, 0:128]), AF.Exp)
    I("act", "activation", w.RW.v(S_[:, 0:128]), ktok.v(S_[:, tt, :]), AF.Copy, scale=bEG.v(S_[:, tt, col:col + 1]))
    I("act", "activation", w.RW.v(S_[:, 128:256]), vtok.v(S_[:, tt, :]), AF.Copy, scale=bcol)
    I("act", "activation", q.kd.v(), ktok.v(S_[:, tt, :]), AF.Copy, scale=kds.v(S_[:, tt, col:col + 1]))
    yield
    I("pool", "tensor_tensor", w.M1.v(), w.F.v(), maskS, ALU.mult)
    if is_lat:
        I("pool", "tensor_tensor", w.E2.v(), w.E2.v(), maskI, ALU.mult)
        I("pool", "tensor_tensor", q.qdT.v(), f32v(qTc), w.EB.v(), ALU.mult)
    yield
    A0 = w.A[0]
    I("dve", "scalar_tensor_tensor", A0.v(), pA.v(S_[:, 128:256]), bcol, w.M1.v(), ALU.mult, ALU.mult)
    if is_lat:
        I("dve", "tensor_tensor", q.qkT.v(), pA.v(S_[:, 256:384]), w.E2.v(), ALU.mult)
    yield
    I("dve", "tensor_tensor", w.AL.v(), A0.v(), NBD, ALU.mult)
    I("dve", "tensor_tensor", A0.v(), A0.v(), BD, ALU.mult)
    yield
    pT = bank
    I("pe", "transpose", pT.v(S_[:, 0:128]), f32v(A0.v()), ident)
    yield
    BP0 = w.BP[0]
    I("act", "activation", BP0.v(S_[:, 0:128]), pT.v(S_[:, 0:128]), AF.Copy)
    yield
    I("dve", "tensor_tensor", BP0.v(S_[:, 128:256]), ident, BP0.v(S_[:, 0:128]), ALU.subtract)
    cur = 0
    for e in (1, 2, 4, 8):
        Ac, An = w.A[cur], w.A[1 - cur]
        BPc, BPn = w.BP[cur], w.BP[1 - cur]
        p1 = bank
        I("pe", "matmul", p1.v(S_[:, 0:128]), BPc.v(S_[:, 0:128]), Ac.v(), start=True, stop=True)
        if e == 1:
            I("pe", "matmul", p1.v(S_[:, 128:256]), Ac.v(), BPc.v(S_[:, 0:128]), start=True, stop=True)
        else:
            I("pe", "matmul", p1.v(S_[:, 128:384]), Ac.v(), BPc.v(), start=True, stop=True)
        yield
        I("act", "activation", An.v(), p1.v(S_[:, 0:128]), AF.Copy)
        I("act", "activation", BPn.v(S_[:, 0:128]), p1.v(S_[:, 128:256]), AF.Copy)
        if e == 1:
            I("act", "activation", BPn.v(S_[:, 128:256]), BPc.v(S_[:, 128:256]), AF.Copy)
        else:
            I("dve", "tensor_tensor", BPn.v(S_[:, 128:256]), p1.v(S_[:, 256:384]), BPc.v(S_[:, 128:256]), ALU.add)
        yield
        cur = 1 - cur
    Ac, BPc, BPn = w.A[cur], w.BP[cur], w.BP[1 - cur]
    p2 = bank
    I("pe", "matmul", p2.v(S_[:, 0:128]), Ac.v(), BPc.v(S_[:, 128:256]), start=True, stop=True)
    yield
    Q = BPn.v(S_[:, 128:256])
    I("dve", "tensor_tensor", Q, p2.v(S_[:, 0:128]), BPc.v(S_[:, 128:256]), ALU.add)
    yield
    p3 = bank
    I("pe", "matmul", p3.v(S_[:, 0:128]), w.AL.v(), Q, start=True, stop=True)
    I("pe", "matmul", p3.v(S_[:, 128:384]), Q, w.RW.v(), start=True, stop=True)
    yield
    I("act", "activation", w.M.v(), p3.v(S_[:, 0:128]), AF.Copy)
    I("act", "activation", w.V1.v(), p3.v(S_[:, 128:384]), AF.Copy)
    yield
    p3a = bank
    I("pe", "matmul", p3a.v(S_[:, 0:256]), w.M.v(), w.V1.v(), start=True, stop=True)
    yield
    I("act", "activation", w.Va.v(), p3a.v(S_[:, 0:256]), AF.Copy)
    yield
    p3b = bank
    I("pe", "matmul", p3b.v(S_[:, 0:256]), w.M.v(), w.Va.v(), start=True, stop=True)
    yield
    I("dve", "tensor_tensor", w.V1.v(), p3b.v(S_[:, 0:256]), w.V1.v(), ALU.add)
    yield
    p3c = bank
    I("pe", "matmul", p3c.v(S_[:, 0:256]), w.M.v(), w.V1.v(), start=True, stop=True)
    yield
    I("dve", "tensor_tensor", q.WU.v(), w.V1.v(), p3c.v(S_[:, 0:256]), ALU.subtract)
    yield
    p3d = bank
    I("pe", "transpose", p3d.v(S_[:, 0:128]), q.WU.v(S_[:, 0:128]), ident)
    yield
    I("act", "activation", q.WT.v(), p3d.v(S_[:, 0:128]), AF.Copy)


def delta_seq(kb, q, bank, ds, d, h, tt, oT, tabs):
    Gam, eG, bEG, kds, glast, btab = tabs
    col = d * 12 + h
    t0 = tok_of_tile(tt)
    is_lat = tt < 16
    I = kb.I
    S = ds.S[ds.si]
    Sn = ds.S[1 - ds.si]
    ds.si = 1 - ds.si
    I("pe", "matmul", bank.v(S_[:, 0:128]), q.WT.v(), S.v(), start=True, stop=True)
    if is_lat:
        I("pe", "matmul", bank.v(S_[:, 128:256]), S.v(), q.qdT.v(), start=True, stop=True)
    yield
    I("dve", "tensor_tensor", q.Ui.v(), q.WU.v(S_[:, 128:256]), bank.v(S_[:, 0:128]), ALU.subtract)
    if is_lat:
        I("act", "activation", q.ob.v(), bank.v(S_[:, 128:256]), AF.Copy)
    yield
    I("pe", "matmul", bank.v(S_[:, 256:384]), q.kd.v(), q.Ui.v(), start=True, stop=True)
    if is_lat:
        I("pe", "matmul", bank.v(S_[:, 384:512]), q.Ui.v(), q.qkT.v(), start=True, stop=True)
    yield
    I("dve", "scalar_tensor_tensor", Sn.v(), S.v(), glast.v(S_[:, tt, col:col + 1]), bank.v(S_[:, 256:384]), ALU.mult, ALU.add)
    if is_lat:
        I("dve", "tensor_tensor", q.ob.v(), bank.v(S_[:, 384:512]), q.ob.v(), ALU.add)
        I("pool", "tensor_tensor", oT.v(S_[:, t0:t0 + 128]), oT.v(S_[:, t0:t0 + 128]), q.ob.v(), ALU.add)
    yield


def run_interleaved(gens):
    act = list(gens)
    while act:
        for g in list(act):
            try:
                next(g)
            except StopIteration:
                act.remove(g)


def build_program(debug=()):
    nc = bass.Bass("TRN2", target_bir_lowering=False)
    kb = KB(nc)
    dbg = set(debug)

    def din(name, shape):
        return Dram(kb, name, shape, F32, kind="ExternalInput")

    def dscr(name, shape, dtype=F32, nbuf=1):
        return Dram(kb, name, shape, dtype, kind=("ExternalOutput" if name in dbg else "Internal"), nbuf=nbuf)

    xT = din("xT", [KC, 128, NT])
    cT = din("cT", [128, KC, 2])
    wmodP = din("wmodP", [36, 128, KC, 512])
    bmodT = din("bmodT", [128, NMOD * KC])
    normsT = din("normsT", [128, 4, KC])
    up1P = din("up1P", [NJ, 128, KC, 256])
    dn1P = din("dn1P", [KC, 128, NJ, 128])
    up2P = din("up2P", [NJ, 128, KC, 256])
    dn2P = din("dn2P", [KC, 128, NJ, 128])
    consts = din("consts", [128, 8, 128])
    outT = Dram(kb, "outT", [KC, 128, SEQ], F32, kind="ExternalOutput", nbuf=1)

    x1T = dscr("x1T", [KC, 128, NT], nbuf=5)
    x3T = dscr("x3T", [KC, 128, SEQ], nbuf=4)
    h2T = dscr("h2T", [KC, 128, NT], BF16, nbuf=5)
    uS = dscr("uS", [4, 128, NT + CTX], nbuf=4)
    winS = din("winS", [128, KC, 560])
    winP = din("winP", [NH, 128, KC, 512])
    convT = din("convT", [128, 36, 9])
    dnnorm = din("dnnorm", [128, 1])
    yT = dscr("yT", [KC, 128, SEQ], BF16, nbuf=KC)
    x2T = dscr("x2T", [KC, 128, SEQ], nbuf=4)
    s5a = din("s5a", [128, 3, 32])
    s5bT = din("s5bT", [2, 32, 16, 64])
    s5cT = din("s5cT", [2, 32, 64, 16])
    s5dT = din("s5dT", [128, 4])
    gluP = din("gluP", [128, 4, 1024])
    woutP = din("woutP", [128, KC, D])
    if "dn_dbg" in dbg:
        dnd = Dram(kb, "dn_dbg", [4, 128, NT], F32, kind="ExternalOutput")
    dnab = din("dnab", [24, 2])

    cst = Tile(kb, "cst", [128, 8, 128], F32)
    kb.dma("sp", cst.v(), consts.v(consts.ap))
    ident = cst.v(S_[:, 0, :])
    ones_bf = Tile(kb, "ones_bf", [128, 128], BF16)
    kb.I("dve", "memset", ones_bf.v(), 1.0)
    epsc = Tile(kb, "epsc", [128, 1], F32)
    kb.I("dve", "memset", epsc.v(), EPS)

    psum = [Tile(kb, f"ps{i}", [128, 512], F32, space="psum") for i in range(8)]
    kb.psum = psum

    Gt = Tile(kb, "Gt", [128, 3, KC, 2], F32)
    St = Tile(kb, "St", [128, 3, KC, 2], F32)
    gate = Tile(kb, "gate", [128, 3, KC, 2], F32)
    nrm = Tile(kb, "nrm", [128, 4, KC], F32)
    kb.dma("sp", nrm.v(), normsT.v(normsT.ap))

    with kb.phase():
        sc0 = Tile(kb, "sc0", [128, KC, 2], F32)
        sc = Tile(kb, "sc", [128, KC, 2], BF16)
        kb.dma("sp", sc0.v(), cT.v(cT.ap))
        kb.I("act", "activation", sc.v(), sc0.v(), AF.Silu)
        wm = [Tile(kb, f"wm{i}", [128, KC, 512], BF16) for i in range(3)]
        mps = psum[0]
        for P in range(36):
            w = wm[P % 3]
            kb.dma("pool", w.v(), wmodP.v(wmodP.ap[P]))
            for j4 in range(4):
                j = 4 * P + j4
                for kc in range(KC):
                    kb.I("pe", "matmul", mps.v(S_[:, 2 * j:2 * j + 2]),
                         w.v(S_[:, kc, j4 * 128:(j4 + 1) * 128]), sc.v(S_[:, kc, :]),
                         start=(kc == 0), stop=(kc == KC - 1))
        bm = Tile(kb, "bm", [128, NMOD * KC], F32)
        kb.dma("sp", bm.v(), bmodT.v(bmodT.ap))
        m = Tile(kb, "m", [128, NMOD * KC, 2], F32)
        kb.I("dve", "tensor_tensor", m.v(),
             View(mps.t[:, 0:288].rearrange("p (j t) -> p j t", t=2), mps.bufs),
             View(bm.t[:, :].unsqueeze(2).to_broadcast([128, NMOD * KC, 2]), bm.bufs), ALU.add)
        for j in range(3):
            r1 = (3 * j + 1) * KC
            kb.I("dve", "tensor_scalar", Gt.v(S_[:, j]), m.v(S_[:, r1:r1 + KC, :]), 1.0, None, ALU.add)
            kb.I("dve", "tensor_tensor", Gt.v(S_[:, j]), Gt.v(S_[:, j]),
                 View(nrm.t[:, j, :].unsqueeze(2).to_broadcast([128, KC, 2]), nrm.bufs), ALU.mult)
            r0 = (3 * j) * KC
            kb.I("dve", "tensor_copy", St.v(S_[:, j]), m.v(S_[:, r0:r0 + KC, :]))
            r2 = (3 * j + 2) * KC
            kb.I("dve", "tensor_scalar", gate.v(S_[:, j]), m.v(S_[:, r2:r2 + KC, :]),
                 (1.0 if j == 1 else 0.5), None, ALU.mult)
        if "m_dbg" in dbg:
            md = Dram(kb, "m_dbg", [128, NMOD * KC, 2], F32, kind="ExternalOutput")
            kb.dma("sp", md.v(md.ap), m.v())

    TB = 512

    TBN = 256

    class NormBufs:
        def __init__(self):
            self.xb = Tile(kb, "xb", [128, KC, TBN], F32)
            self.sq = Tile(kb, "sq", [128, KC, TBN], BF16)
            self.rstd = Tile(kb, "rstd", [128, TBN], F32)
            self.tmpn = [Tile(kb, f"tmpn{i}", [128, TBN], F32) for i in range(2)]

    def norm_mod(nb, src, t0, tb, j, t, dst, G=None, Sv=None):
        xb, sq, rstd, tmpn = nb.xb, nb.sq, nb.rstd, nb.tmpn
        for h0 in range(0, tb, TBN):
            hb = min(TBN, tb - h0)
            kb.dma("sp", xb.v(S_[:, :, 0:hb]), src.v(src.ap[:, :, t0 + h0:t0 + h0 + hb].rearrange("k p t -> p k t")))
            kb.I("act", "activation", sq.v(S_[:, :, 0:hb]), xb.v(S_[:, :, 0:hb]), AF.Square)
            pss = psum[7]
            for kc in range(KC):
                kb.I("pe", "matmul", pss.v(S_[:, 0:hb]), ones_bf.v(), sq.v(S_[:, kc, 0:hb]),
                     start=(kc == 0), stop=(kc == KC - 1))
            kb.I("act", "activation", rstd.v(S_[:, 0:hb]), pss.v(S_[:, 0:hb]), AF.Sqrt, bias=epsc.v(), scale=1.0 / D)
            kb.I("dve", "reciprocal", rstd.v(S_[:, 0:hb]), rstd.v(S_[:, 0:hb]))
            for kc in range(KC):
                tm = tmpn[kc % 2]
                gv = Gt.v(S_[:, j, kc, t:t + 1]) if G is None else G(kc)
                kb.I("dve", "scalar_tensor_tensor", tm.v(S_[:, 0:hb]), xb.v(S_[:, kc, 0:hb]),
                     gv, rstd.v(S_[:, 0:hb]), ALU.mult, ALU.mult)
                if Sv is None:
                    kb.I("act", "activation", dst(kc, h0, h0 + hb), tm.v(S_[:, 0:hb]), AF.Identity,
                         bias=St.v(S_[:, j, kc, t:t + 1]), scale=1.0)
                else:
                    kb.I("act", "activation", dst(kc, h0, h0 + hb), tm.v(S_[:, 0:hb]), AF.Copy)

    class FfnBufs:
        def __init__(self):
            self.hTs = [Tile(kb, f"hT{i}", [128, KC, TB], BF16, nbuf=KC) for i in range(2)]
            self.upt = [Tile(kb, f"upt{i}", [128, KC, 256], BF16) for i in range(3)]
            self.dnt = [Tile(kb, f"dnt{i}", [128, NJ, 128], BF16) for i in range(3)]
            self.actb = Tile(kb, "actb", [128, NJ, TB], BF16, nbuf=NJ)
            self.sg = [Tile(kb, f"sg{i}", [128, TB], F32) for i in range(2)]
            self.xres = [Tile(kb, f"xres{i}", [128, TB], F32) for i in range(2)]
            self.xo = [Tile(kb, f"xo{i}", [128, TB], F32) for i in range(2)]

    def ffn_up(fb, hT, upP, tb):
        upt, actb, sg = fb.upt, fb.actb, fb.sg

        def load_up(jj):
            kb.dma("pool", upt[jj % 3].v(), upP.v(upP.ap[jj]))
        load_up(0)
        load_up(1)
        for jj in range(NJ):
            if jj + 2 < NJ:
                load_up(jj + 2)
            w = upt[jj % 3]
            pg = psum[(2 * jj) % 4]
            pu = psum[(2 * jj + 1) % 4]
            for half, pt in ((0, pg), (1, pu)):
                for kc in range(KC):
                    kb.I("pe", "matmul", pt.v(S_[:, 0:tb]), w.v(S_[:, kc, half * 128:(half + 1) * 128]),
                         hT.v(S_[:, kc, 0:tb], kc), start=(kc == 0), stop=(kc == KC - 1))
            s_ = sg[jj % 2]
            kb.I("act", "activation", s_.v(S_[:, 0:tb]), pg.v(S_[:, 0:tb]), AF.Silu)
            kb.I("dve", "tensor_tensor", actb.v(S_[:, jj, 0:tb], jj), s_.v(S_[:, 0:tb]), pu.v(S_[:, 0:tb]), ALU.mult)

    def ffn_down(fb, dnP, src, dst, dst_buf, t0, tb, j, t):
        dnt, actb, xres, xo = fb.dnt, fb.actb, fb.xres, fb.xo

        def load_dn(dc):
            kb.dma("pool", dnt[dc % 3].v(), dnP.v(dnP.ap[dc]))
        load_dn(0)
        load_dn(1)
        for dc in range(KC):
            if dc + 2 < KC:
                load_dn(dc + 2)
            w = dnt[dc % 3]
            po = psum[4 + dc % 2]
            xr = xres[dc % 2]
            kb.dma("sp", xr.v(S_[:, 0:tb]), src.v(src.ap[dc, :, t0:t0 + tb]))
            for jj in range(NJ):
                kb.I("pe", "matmul", po.v(S_[:, 0:tb]), w.v(S_[:, jj, :]), actb.v(S_[:, jj, 0:tb], jj),
                     start=(jj == 0), stop=(jj == NJ - 1))
            o = xo[dc % 2]
            kb.I("dve", "scalar_tensor_tensor", o.v(S_[:, 0:tb]), po.v(S_[:, 0:tb]),
                 gate.v(S_[:, j, dc, t:t + 1]), xr.v(S_[:, 0:tb]), ALU.mult, ALU.add)
            kb.dma("sp", dst.v(dst.ap[dc, :, t0:t0 + tb], dst_buf), o.v(S_[:, 0:tb]), adds=True)

    def ffn_phase(upP, dnP, src, dst, j, blks):
        nb = NormBufs()
        fb = FfnBufs()
        hdst = lambda i: (lambda kc, a, b: fb.hTs[i % 2].v(S_[:, kc, a:b], kc))
        t0, tb, t = blks[0]
        norm_mod(nb, src, t0, tb, j, t, hdst(0))
        for bi, (t0, tb, t) in enumerate(blks):
            ffn_up(fb, fb.hTs[bi % 2], upP, tb)
            if bi + 1 < len(blks):
                n0, nbk, nt_ = blks[bi + 1]
                norm_mod(nb, src, n0, nbk, j, nt_, hdst(bi + 1))
            ffn_down(fb, dnP, src, dst, bi, t0, tb, j, t)

    blocks = [(i * TB, TB, 0) for i in range(SEQ // TB)] + [(SEQ, CTX, 1)]
    if STOP_AFTER >= 1:
        with kb.phase():
            ffn_phase(up1P, dn1P, xT, x1T, 0, blocks[:NBLK])

    NTT = NT // 128
    Gam = Tile(kb, "Gam", [128, NTT, 24], F32)
    eG = Tile(kb, "eG", [128, NTT, 24], F32)
    bEG = Tile(kb, "bEG", [128, NTT, 24], F32)
    kds = Tile(kb, "kds", [128, NTT, 24], F32)
    glast = Tile(kb, "glast", [128, NTT, 24], F32)
    btab = Tile(kb, "btab", [128, NTT, 24], F32)
    LinclV, UinclV = cst.v(S_[:, 2, :]), cst.v(S_[:, 3, :])
    LstrV, UstrV = cst.v(S_[:, 4, :]), cst.v(S_[:, 5, :])
    onesV = cst.v(S_[:, 1, :])
    if STOP_AFTER >= 2:
        with kb.phase():
            nb = NormBufs()
            h2 = Tile(kb, "h2", [128, KC, TB], BF16, nbuf=KC)
            wS = Tile(kb, "wS", [128, KC, 560], BF16)
            kb.dma("pool", wS.v(), winS.v(winS.ap))
            graw = Tile(kb, "graw", [24, NT], F32)
            braw = Tile(kb, "braw", [24, NT], F32)
            useq = Tile(kb, "useq", [128, 4, NT + CTX], F32, nbuf=4)
            ab = Tile(kb, "ab", [24, 2], F32)
            kb.dma("sp", ab.v(), dnab.v(dnab.ap))
            for bi, (t0, tb, t) in enumerate(blocks):
                norm_mod(nb, x1T, t0, tb, 1, t, lambda kc, a, b: h2.v(S_[:, kc, a:b], kc))
                kb.dma("sp", h2T.v(h2T.ap[:, :, t0:t0 + tb].rearrange("k p t -> p k t"), bi), h2.v(S_[:, :, 0:tb]), adds=True)
                pa, pb = psum[0], psum[1]
                for (pt, c0) in ((pa, 0), (pb, 24)):
                    for kc in range(KC):
                        kb.I("pe", "matmul", pt.v(S_[0:24, 0:tb]), wS.v(S_[:, kc, c0:c0 + 24]), h2.v(S_[:, kc, 0:tb], kc),
                             start=(kc == 0), stop=(kc == KC - 1))
                kb.I("act", "activation", graw.v(S_[:, t0:t0 + tb]), pa.v(S_[0:24, 0:tb]), AF.Copy)
                kb.I("dve", "tensor_copy", braw.v(S_[:, t0:t0 + tb]), pb.v(S_[0:24, 0:tb]))
                for uc in range(4):
                    pu = psum[2 + uc % 2]
                    for kc in range(KC):
                        kb.I("pe", "matmul", pu.v(S_[:, 0:tb]), wS.v(S_[:, kc, 48 + uc * 128:48 + (uc + 1) * 128]),
                             h2.v(S_[:, kc, 0:tb], kc), start=(kc == 0), stop=(kc == KC - 1))
                    if t == 1:
                        kb.I("act", "activation", useq.v(S_[:, uc, 0:CTX], uc), pu.v(S_[:, 0:tb]), AF.Copy)
                        kb.I("act", "activation", useq.v(S_[:, uc, NT:NT + CTX], uc), pu.v(S_[:, 0:tb]), AF.Copy)
                    else:
                        r0 = t0 // 64
                        nr = tb // 64
                        dst = useq.t[:, uc, CTX:NT].rearrange("p (c r) -> p r c", r=32)[:, r0:r0 + nr, :]
                        src = pu.t[:, 0:tb].rearrange("p (r c) -> p r c", c=64)
                        kb.I("act", "activation", View(dst, [useq.bufs[uc]]), View(src, pu.bufs), AF.Copy)
            for uc in range(4):
                kb.dma("sp", uS.v(uS.ap[uc], uc), useq.v(S_[:, uc, :], uc))
            nea = Tile(kb, "nea", [24, 1], F32)
            kb.I("act", "activation", nea.v(), ab.v(S_[:, 0:1]), AF.Exp)
            kb.I("dve", "tensor_scalar", nea.v(), nea.v(), -1.0, None, ALU.mult)
            kb.I("act", "activation", graw.v(), graw.v(), AF.Exp, bias=ab.v(S_[:, 1:2]), scale=1.0)
            kb.I("act", "activation", graw.v(), graw.v(), AF.Ln, bias=1.0, scale=1.0)
            kb.I("dve", "tensor_scalar", graw.v(), graw.v(), nea.v(S_[:, 0:1]), None, ALU.mult)
            kb.I("act", "activation", braw.v(), braw.v(), AF.Sigmoid)
            gtok = Tile(kb, "gtok", [128, NTT, 24], F32)
            tot = Tile(kb, "tot", [128, NTT, 24], F32)
            for tt in range(NTT):
                pg, pbb = psum[0], psum[1]
                kb.I("pe", "transpose", pg.v(S_[:, 0:24]), graw.v(S_[:, tt * 128:(tt + 1) * 128]), cst.v(S_[0:24, 0, 0:24]))
                kb.I("pe", "transpose", pbb.v(S_[:, 0:24]), braw.v(S_[:, tt * 128:(tt + 1) * 128]), cst.v(S_[0:24, 0, 0:24]))
                kb.I("act", "activation", gtok.v(S_[:, tt, :]), pg.v(S_[:, 0:24]), AF.Copy)
                kb.I("dve", "tensor_copy", btab.v(S_[:, tt, :]), pbb.v(S_[:, 0:24]))
                pc, pd = psum[2], psum[3]
                kb.I("pe", "matmul", pc.v(S_[:, 0:12]), LinclV, gtok.v(S_[:, tt, 0:12]), start=True, stop=True)
                kb.I("pe", "matmul", pc.v(S_[:, 12:24]), UinclV, gtok.v(S_[:, tt, 12:24]), start=True, stop=True)
                kb.I("pe", "matmul", pd.v(S_[:, 0:24]), onesV, gtok.v(S_[:, tt, :]), start=True, stop=True)
                kb.I("act", "activation", Gam.v(S_[:, tt, :]), pc.v(S_[:, 0:24]), AF.Copy)
                kb.I("dve", "tensor_copy", tot.v(S_[:, tt, :]), pd.v(S_[:, 0:24]))
            kb.I("act", "activation", eG.v(), Gam.v(), AF.Exp)
            kb.I("dve", "tensor_tensor", bEG.v(), eG.v(), btab.v(), ALU.mult)
            kb.I("dve", "tensor_tensor", kds.v(), tot.v(), Gam.v(), ALU.subtract)
            kb.I("act", "activation", kds.v(), kds.v(), AF.Exp)
            kb.I("act", "activation", glast.v(), tot.v(), AF.Exp)
            if "tab_dbg" in dbg:
                td = Dram(kb, "tab_dbg", [6, 128, NTT, 24], F32, kind="ExternalOutput")
                for i, tl in enumerate([Gam, eG, bEG, kds, glast, btab]):
                    kb.dma("sp", td.v(td.ap[i]), tl.v(), adds=True)

    GB = 3
    if STOP_AFTER >= 3:
        with kb.phase():
            PADL = 65
            XW = PADL + SEQ + PADL
            vT = Tile(kb, "vT", [128, NT], F32)
            KQ = Tile(kb, "KQ", [128, NTT, 256], F32R)
            zT = Tile(kb, "zT", [128, SEQ], F32)
            ktok = Tile(kb, "ktok", [128, NTT, 128], F32)
            vtok = Tile(kb, "vtok", [128, NTT, 128], F32)
            oT = Tile(kb, "oT", [128, SEQ], F32)
            ybf = Tile(kb, "ybf", [128, SEQ], BF16)
            convw = Tile(kb, "convw", [128, 36, 9], F32)
            kb.dma("sp", convw.v(), convT.v(convT.ap))
            dnw = Tile(kb, "dnw", [128, 1], F32)
            kb.dma("sp", dnw.v(), dnnorm.v(dnnorm.ap))
            rinv = Tile(kb, "rinv", [128, 512], F32)
            tabs = (Gam, eG, bEG, kds, glast, btab)
            cmv = (ident, LinclV, UinclV, LstrV, UstrV, cst.v(S_[:, 6, :]), cst.v(S_[:, 7, :]))

            class WS:
                def __init__(self, n):
                    mk = lambda nm, w=128, dt=F32: Tile(kb, f"{nm}{n}", [128, w], dt)
                    self.F, self.EB, self.M1, self.E2 = mk("F"), mk("EB"), mk("M1"), mk("E2")
                    self.A = [mk("A0", 128, F32R), mk("A1", 128, F32R)]
                    self.BP = [mk("BP0", 256, F32R), mk("BP1", 256, F32R)]
                    self.RW, self.V1, self.Va = mk("RW", 256, F32R), mk("V1", 256, F32R), mk("Va", 256, F32R)
                    self.AL, self.M = mk("AL", 128, F32R), mk("M", 128, F32R)

            class WQ:
                def __init__(self, n):
                    mk = lambda nm, w=128: Tile(kb, f"{nm}{n}", [128, w], F32)
                    self.qkT, self.qdT, self.kd, self.WT, self.Ui = mk("qkT"), mk("qdT"), mk("kd"), mk("WT"), mk("Ui")
                    self.ob = mk("ob")
                    self.WU = mk("WU", 256)

            class DS:
                def __init__(self, d):
                    self.S = [Tile(kb, f"S{d}_{i}", [128, 128], F32) for i in range(2)]
                    self.si = 0

            for h in range(NHEADS):
                with kb.phase():
                    winh = Tile(kb, "winh", [128, KC, 512], BF16)
                    h2b = [Tile(kb, f"h2b{i}", [128, KC, 256], BF16) for i in range(2)]
                    Xs = [[Tile(kb, f"X{n}{i}", [128, XW], F32) for n in ("0", "L", "R")] for i in range(2)]
                    C0 = [Tile(kb, f"C0{i}", [128, CTX + 2], F32) for i in range(3)]
                    sqs = Tile(kb, "sqs", [128, NT], F32)
                    qacc = Tile(kb, "qacc", [128, NT], F32)
                    kacc = Tile(kb, "kacc", [128, NT], F32)
                    qkvT = [qacc, kacc, vT]
                    kb.dma("pool", winh.v(), winP.v(winP.ap[h]))
                    for tl in Xs[0] + Xs[1] + C0:
                        kb.I("pool", "memset", tl.v(), 0.0)
                    nblk = NT // 256
                    nld = 0
                    for comp in range(4):
                        x0, XL, XR = Xs[comp % 2]
                        for bi in range(nblk):
                            t0 = bi * 256
                            if comp == 3 and t0 >= SEQ:
                                continue
                            hb = h2b[nld % 2]
                            nld += 1
                            kb.dma("sp", hb.v(), h2T.v(h2T.ap[:, :, t0:t0 + 256].rearrange("k p t -> p k t")))
                            pt = kb.bank()
                            for kc in range(KC):
                                kb.I("pe", "matmul", pt.v(S_[:, 0:256]), winh.v(S_[:, kc, comp * 128:(comp + 1) * 128]),
                                     hb.v(S_[:, kc, :]), start=(kc == 0), stop=(kc == KC - 1))
                            if comp == 3:
                                kb.I("act", "activation", zT.v(S_[:, t0:t0 + 256]), pt.v(S_[:, 0:256]), AF.Silu)
                            elif t0 < SEQ:
                                kb.I("act", "activation", x0.v(S_[:, PADL + t0:PADL + t0 + 256]), pt.v(S_[:, 0:256]), AF.Copy)
                                p3v = lambda c0, c1: View(pt.t[:, 0:256].rearrange("p (r c) -> p r c", c=64)[:, :, c0:c1], pt.bufs)
                                x3v = lambda tl, c0, c1: View(tl.t[:, PADL + t0:PADL + t0 + 256].rearrange("p (r c) -> p r c", c=64)[:, :, c0:c1], tl.bufs)
                                kb.I("act", "activation", x3v(XL, 0, 63), p3v(0, 63), AF.Copy)
                                kb.I("act", "activation", x3v(XR, 1, 64), p3v(1, 64), AF.Copy)
                            else:
                                kb.I("act", "activation", C0[comp].v(S_[:, 1:1 + CTX]), pt.v(S_[:, 0:256]), AF.Copy)
                        if comp == 3:
                            continue
                        acc = qkvT[comp]
                        ch = comp * 12 + h
                        srcs = (XL, x0, XR)
                        order = [(1, 1)] + [(i, j) for i in range(3) for j in range(3) if (i, j) != (1, 1)]
                        for n, (i, j) in enumerate(order):
                            off = PADL + (i - 1) * 64 + (j - 1)
                            sv = srcs[j].v(S_[:, off:off + SEQ])
                            wv = convw.v(S_[:, ch, i * 3 + j:i * 3 + j + 1])
                            if n == 0:
                                kb.I("dve", "tensor_scalar", acc.v(S_[:, 0:SEQ]), sv, wv, None, ALU.mult)
                            else:
                                kb.I("dve", "scalar_tensor_tensor", acc.v(S_[:, 0:SEQ]), sv, wv, acc.v(S_[:, 0:SEQ]), ALU.mult, ALU.add)
                        for n, j in enumerate((1, 0, 2)):
                            sv = C0[comp].v(S_[:, j:j + CTX])
                            wv = convw.v(S_[:, ch, 3 + j:3 + j + 1])
                            if n == 0:
                                kb.I("dve", "tensor_scalar", acc.v(S_[:, SEQ:NT]), sv, wv, None, ALU.mult)
                            else:
                                kb.I("dve", "scalar_tensor_tensor", acc.v(S_[:, SEQ:NT]), sv, wv, acc.v(S_[:, SEQ:NT]), ALU.mult, ALU.add)
                        kb.I("act", "activation", acc.v(), acc.v(), AF.Silu)
                        if comp < 2:
                            kb.I("act", "activation", sqs.v(), acc.v(), AF.Square)
                            for b0 in range(0, NT, 512):
                                bw = min(512, NT - b0)
                                pt = kb.bank()
                                kb.I("pe", "matmul", pt.v(S_[:, 0:bw]), onesV, sqs.v(S_[:, b0:b0 + bw]), start=True, stop=True)
                                kb.I("act", "activation", rinv.v(S_[:, 0:bw]), pt.v(S_[:, 0:bw]), AF.Sqrt, bias=epsc.v(), scale=1.0)
                                kb.I("dve", "reciprocal", rinv.v(S_[:, 0:bw]), rinv.v(S_[:, 0:bw]))
                                nt4 = bw // 128
                                tl0 = b0 // 128
                                v3_ = lambda ap: ap.rearrange("p (a b) -> p a b", b=128)
                                a3 = View(v3_(acc.t[:, b0:b0 + bw]), acc.bufs)
                                r3 = View(v3_(rinv.t[:, 0:bw]), rinv.bufs)
                                if comp == 0:
                                    kb.I("dve", "scalar_tensor_tensor", KQ.v(S_[:, tl0:tl0 + nt4, 128:256]), a3,
                                         128.0 ** -0.5, r3, ALU.mult, ALU.mult)
                                else:
                                    kb.I("dve", "tensor_tensor", KQ.v(S_[:, tl0:tl0 + nt4, 0:128]), a3, r3, ALU.mult)
                    if "dn_dbg" in dbg and h == DBG_H:
                        kb.dma("sp", dnd.v(dnd.ap[2]), vT.v(), adds=True)
                    for (which, dstT) in ((0, ktok), (1, vtok)):
                        for t4 in range(0, NTT, 4):
                            n4 = min(4, NTT - t4)
                            pt = kb.bank()
                            for q4 in range(n4):
                                tt = t4 + q4
                                tok0 = tok_of_tile(tt)
                                srcv = f32v(KQ.v(S_[:, tt, 0:128])) if which == 0 else vT.v(S_[:, tok0:tok0 + 128])
                                kb.I("pe", "transpose", pt.v(S_[:, q4 * 128:(q4 + 1) * 128]), srcv, ident)
                            kb.I("act", "activation", dstT.v(S_[:, t4:t4 + n4, :]),
                                 View(pt.t[:, 0:n4 * 128].rearrange("p (a b) -> p a b", b=128), pt.bufs), AF.Copy)
                with kb.phase():
                    ws = [[WS(f"_{d}{g}") for g in range(GB)] for d in range(2)]
                    wq = [[[WQ(f"_{p}{d}{g}") for g in range(GB)] for d in range(2)] for p in range(2)]
                    dss = [DS(0), DS(1)]
                    sq2 = Tile(kb, "sq2", [128, SEQ], F32)
                    kb.I("pool", "memset", oT.v(), 0.0)
                    for d in range(2):
                        kb.I("pool", "memset", dss[d].S[0].v(), 0.0)
                    orders = [[16, 17] + list(range(16)), [17, 16] + list(range(15, -1, -1))]
                    nb = NTT // GB

                    def pre_gens(b):
                        gl = []
                        for g in range(GB):
                            for d in range(2):
                                gl.append(delta_pre(kb, ws[d][g], wq[b % 2][d][g], psum[g * 2 + d], d, h, orders[d][b * GB + g], KQ, ktok, vtok, tabs, cmv))
                        return gl

                    def seq_stream(b, d):
                        for g in range(GB):
                            yield from delta_seq(kb, wq[b % 2][d][g], psum[6 + d], dss[d], d, h, orders[d][b * GB + g], oT, tabs)
                    if OVERLAP:
                        run_interleaved(pre_gens(0))
                        for b in range(1, nb):
                            run_interleaved([seq_stream(b - 1, 0), seq_stream(b - 1, 1)] + pre_gens(b))
                        run_interleaved([seq_stream(nb - 1, 0), seq_stream(nb - 1, 1)])
                    else:
                        for b in range(nb):
                            run_interleaved(pre_gens(b))
                            run_interleaved([seq_stream(b, 0), seq_stream(b, 1)])
                    if "dn_dbg" in dbg and h == DBG_H:
                        kb.dma("sp", dnd.v(dnd.ap[3, :, 0:SEQ]), oT.v(), adds=True)
                    kb.I("act", "activation", sq2.v(), oT.v(), AF.Square)
                    for b0 in range(0, SEQ, 512):
                        pt = kb.bank()
                        kb.I("pe", "matmul", pt.v(), onesV, sq2.v(S_[:, b0:b0 + 512]), start=True, stop=True)
                        kb.I("act", "activation", rinv.v(), pt.v(), AF.Sqrt, bias=epsc.v(), scale=1.0 / 128)
                        kb.I("dve", "reciprocal", rinv.v(), rinv.v())
                        kb.I("dve", "tensor_tensor", oT.v(S_[:, b0:b0 + 512]), oT.v(S_[:, b0:b0 + 512]), rinv.v(), ALU.mult)
                        kb.I("dve", "scalar_tensor_tensor", ybf.v(S_[:, b0:b0 + 512]), oT.v(S_[:, b0:b0 + 512]), dnw.v(),
                             zT.v(S_[:, b0:b0 + 512]), ALU.mult, ALU.mult)
                    kb.dma("sp", yT.v(yT.ap[h], h), ybf.v())

    if STOP_AFTER >= 4:
        with kb.phase():
            useq = Tile(kb, "useq", [128, 4, NT + CTX], F32, nbuf=4)
            for uc in range(4):
                kb.dma("sp", useq.v(S_[:, uc, :], uc), uS.v(uS.ap[uc], uc))
            sp_ = Tile(kb, "s5p", [128, 3, 32], F32)
            kb.dma("sp", sp_.v(), s5a.v(s5a.ap))
            mk = lambda n: Tile(kb, n, [128, 32], F32)
            dtt, ar, th, rr, sn, cs, tmpa, tmpb, nr_, ni_, den, cr, ci, nci = [mk(f"s5s{i}") for i in range(14)]
            aRe, aIm, ldt = sp_.v(S_[:, 0, :]), sp_.v(S_[:, 1, :]), sp_.v(S_[:, 2, :])
            V = "dve"
            kb.I("act", "activation", dtt.v(), ldt, AF.Exp)
            kb.I(V, "tensor_tensor", ar.v(), aRe, dtt.v(), ALU.mult)
            kb.I(V, "tensor_tensor", th.v(), aIm, dtt.v(), ALU.mult)
            kb.I("act", "activation", rr.v(), ar.v(), AF.Exp)

            def sin_of(dst, src, shift):
                kb.I(V, "tensor_scalar", dst.v(), src.v(), float(shift), None, ALU.add)
                for _ in range(5):
                    kb.I(V, "tensor_scalar", tmpa.v(), dst.v(), float(np.pi), float(2 * np.pi), ALU.is_gt, ALU.mult)
                    kb.I(V, "tensor_tensor", dst.v(), dst.v(), tmpa.v(), ALU.subtract)
                kb.I("act", "activation", dst.v(), dst.v(), AF.Sin)
            sin_of(sn, th, 0.0)
            sin_of(cs, th, np.pi / 2)
            kb.I(V, "tensor_tensor", nr_.v(), rr.v(), cs.v(), ALU.mult)
            kb.I(V, "tensor_scalar", nr_.v(), nr_.v(), -1.0, None, ALU.add)
            kb.I(V, "tensor_tensor", ni_.v(), rr.v(), sn.v(), ALU.mult)
            kb.I(V, "tensor_tensor", den.v(), aRe, aRe, ALU.mult)
            kb.I(V, "tensor_tensor", tmpa.v(), aIm, aIm, ALU.mult)
            kb.I(V, "tensor_tensor", den.v(), den.v(), tmpa.v(), ALU.add)
            kb.I(V, "reciprocal", den.v(), den.v())
            kb.I(V, "tensor_tensor", cr.v(), nr_.v(), aRe, ALU.mult)
            kb.I(V, "tensor_tensor", tmpa.v(), ni_.v(), aIm, ALU.mult)
            kb.I(V, "tensor_tensor", cr.v(), cr.v(), tmpa.v(), ALU.add)
            kb.I(V, "tensor_tensor", cr.v(), cr.v(), den.v(), ALU.mult)
            kb.I(V, "tensor_tensor", ci.v(), ni_.v(), aRe, ALU.mult)
            kb.I(V, "tensor_tensor", tmpa.v(), nr_.v(), aIm, ALU.mult)
            kb.I(V, "tensor_tensor", ci.v(), ci.v(), tmpa.v(), ALU.subtract)
            kb.I(V, "tensor_tensor", ci.v(), ci.v(), den.v(), ALU.mult)
            kb.I(V, "tensor_scalar", nci.v(), ci.v(), -1.0, None, ALU.mult)
            NPW = 12
            wpow = Tile(kb, "wpow", [128, NPW, 2, 32], F32)
            kb.I(V, "tensor_copy", wpow.v(S_[:, 0, 0, :]), cs.v())
            kb.I(V, "tensor_scalar", wpow.v(S_[:, 0, 1, :]), sn.v(), -1.0, None, ALU.mult)
            for k in range(NPW - 1):
                x_, y_ = wpow.v(S_[:, k, 0, :]), wpow.v(S_[:, k, 1, :])
                kb.I(V, "tensor_tensor", tmpa.v(), x_, x_, ALU.mult)
                kb.I(V, "tensor_tensor", tmpb.v(), y_, y_, ALU.mult)
                kb.I(V, "tensor_tensor", wpow.v(S_[:, k + 1, 0, :]), tmpa.v(), tmpb.v(), ALU.subtract)
                kb.I(V, "tensor_tensor", tmpa.v(), x_, y_, ALU.mult)
                kb.I(V, "tensor_scalar", wpow.v(S_[:, k + 1, 1, :]), tmpa.v(), 2.0, None, ALU.mult)
            NS = 48
            LOf = Tile(kb, "LOf", [128, 2, 32, NS], F32)
            HIf = Tile(kb, "HIf", [128, 2, 32, NS], F32)
            with kb.phase():
                cmul_t = [Tile(kb, f"cmt{i}", [128, 32, NS], F32) for i in range(2)]
                Tt = Tile(kb, "Tt", [128, 2, 32, NS], F32)

                def cpow_table(T, wp):
                    kb.I(V, "memset", T.v(S_[:, 0, :, 0:1]), 1.0)
                    kb.I(V, "memset", T.v(S_[:, 1, :, 0:1]), 0.0)
                    for j in range(6):
                        n = 1 << j
                        seg = min(n, NS - n)
                        wr, wi = wp[j]
                        bc = lambda v: View(v.ap.unsqueeze(2).to_broadcast([128, 32, seg]), v.bufs)
                        a_re, a_im = T.v(S_[:, 0, :, 0:seg]), T.v(S_[:, 1, :, 0:seg])
                        ta, tb_ = cmul_t[0].v(S_[:, :, 0:seg]), cmul_t[1].v(S_[:, :, 0:seg])
                        kb.I(V, "tensor_tensor", ta, a_im, bc(wi), ALU.mult)
                        kb.I(V, "tensor_tensor", tb_, a_re, bc(wr), ALU.mult)
                        kb.I(V, "tensor_tensor", T.v(S_[:, 0, :, n:n + seg]), tb_, ta, ALU.subtract)
                        kb.I(V, "tensor_tensor", ta, a_im, bc(wr), ALU.mult)
                        kb.I(V, "tensor_tensor", tb_, a_re, bc(wi), ALU.mult)
                        kb.I(V, "tensor_tensor", T.v(S_[:, 1, :, n:n + seg]), tb_, ta, ALU.add)

                def finalize(dst):
                    for ri in range(2):
                        kb.I(V, "tensor_copy", dst.v(S_[:, ri, 0:16, :]), Tt.v(S_[:, ri, 0:16, :]))
                        kb.I(V, "tensor_copy", dst.v(S_[:, ri, 16:32, :]), View(Tt.t[:, ri, 16:32, ::-1], Tt.bufs))
                cpow_table(Tt, [(wpow.v(S_[:, j, 0, :]), wpow.v(S_[:, j, 1, :])) for j in range(6)])
                finalize(LOf)
                w48 = Tile(kb, "w48", [128, 6, 2, 32], F32)
                x5, y5, x4, y4 = (wpow.v(S_[:, 5, 0, :]), wpow.v(S_[:, 5, 1, :]), wpow.v(S_[:, 4, 0, :]), wpow.v(S_[:, 4, 1, :]))
                kb.I(V, "tensor_tensor", tmpa.v(), x5, x4, ALU.mult)
                kb.I(V, "tensor_tensor", tmpb.v(), y5, y4, ALU.mult)
                kb.I(V, "tensor_tensor", w48.v(S_[:, 0, 0, :]), tmpa.v(), tmpb.v(), ALU.subtract)
                kb.I(V, "tensor_tensor", tmpa.v(), x5, y4, ALU.mult)
                kb.I(V, "tensor_tensor", tmpb.v(), y5, x4, ALU.mult)
                kb.I(V, "tensor_tensor", w48.v(S_[:, 0, 1, :]), tmpa.v(), tmpb.v(), ALU.add)
                for k in range(5):
                    x_, y_ = w48.v(S_[:, k, 0, :]), w48.v(S_[:, k, 1, :])
                    kb.I(V, "tensor_tensor", tmpa.v(), x_, x_, ALU.mult)
                    kb.I(V, "tensor_tensor", tmpb.v(), y_, y_, ALU.mult)
                    kb.I(V, "tensor_tensor", w48.v(S_[:, k + 1, 0, :]), tmpa.v(), tmpb.v(), ALU.subtract)
                    kb.I(V, "tensor_tensor", tmpa.v(), x_, y_, ALU.mult)
                    kb.I(V, "tensor_scalar", w48.v(S_[:, k + 1, 1, :]), tmpa.v(), 2.0, None, ALU.mult)
                cpow_table(Tt, [(w48.v(S_[:, j, 0, :]), w48.v(S_[:, j, 1, :])) for j in range(6)])
                finalize(HIf)
            Bpad = Tile(kb, "Bpad", [128, 4, 2, 128], F32)
            Cpad = Tile(kb, "Cpad", [128, 4, 2, 128], F32)

            def load_pads(uc):
                kb.I("pool", "memset", Bpad.v(), 0.0)
                kb.I("pool", "memset", Cpad.v(), 0.0)
                for g in range(uc * 8, uc * 8 + 8):
                    cl, e, lg = (g // 2) % 4, g % 2, g % 8
                    for ri in range(2):
                        kb.dma("sp", Bpad.v(S_[lg * 16:(lg + 1) * 16, cl, ri, e * 64:(e + 1) * 64]), s5bT.v(s5bT.ap[ri, g]), adds=True)
                        kb.dma("sp", Cpad.v(S_[e * 64:(e + 1) * 64, cl, ri, lg * 16:(lg + 1) * 16]), s5cT.v(s5cT.ap[ri, g]), adds=True)
            dcol = Tile(kb, "dcol", [128, 4], F32)
            kb.dma("sp", dcol.v(), s5dT.v(s5dT.ap))
            gluw = Tile(kb, "gluw", [128, 4, 1024], BF16)
            kb.dma("pool", gluw.v(), gluP.v(gluP.ap))
            Rre, Rim = Tile(kb, "Rre", [128, NT], F32), Tile(kb, "Rim", [128, NT], F32)
            Zre, Zim = Tile(kb, "Zre", [128, NT], F32), Tile(kb, "Zim", [128, NT], F32)
            Sre, Sim = Tile(kb, "Sre", [128, NT], F32), Tile(kb, "Sim", [128, NT], F32)
            t1, t2 = Sre, Sim
            bre = [Tile(kb, f"bre{i}", [128, 512], F32) for i in range(2)]
            bim = [Tile(kb, f"bim{i}", [128, 512], F32) for i in range(2)]
            CeR = [Tile(kb, f"CeR{i}", [128, 128], F32) for i in range(2)]
            CeI = [Tile(kb, f"CeI{i}", [128, 128], F32) for i in range(2)]
            ct = Tile(kb, "s5ct", [128, 128], F32)
            ysb, gt1 = t2, t1
            ge = Tile(kb, "ge", [128, 4, SEQ], BF16, nbuf=4)
            it = 0
            if "r_dbg" in dbg:
                rdb = Dram(kb, "r_dbg", [2, 2, 128, NT], F32, kind="ExternalOutput")
                ldb = Dram(kb, "lo_dbg", [2, 128, 2, 32, NS], F32, kind="ExternalOutput")
                kb.dma("sp", ldb.v(ldb.ap[0]), LOf.v(), adds=True)
                kb.dma("sp", ldb.v(ldb.ap[1]), HIf.v(), adds=True)
            for uc in range(4):
                yps = psum[0:4]
                load_pads(uc)
                for ccl in range(4):
                    cc = uc * 4 + ccl
                    for d in range(2):
                        col = d * 16 + cc
                        off = 0 if d == 0 else CTX
                        lat0 = CTX - off
                        c1 = lambda tl: tl.v(S_[:, col:col + 1])
                        Hs, Ls = HIf, LOf
                        hb = lambda ri: View(Hs.t[:, ri, col, :].unsqueeze(2).to_broadcast([128, NS, NS]), Hs.bufs)
                        lb = lambda ri: View(Ls.t[:, ri, col, :].unsqueeze(1).to_broadcast([128, NS, NS]), Ls.bufs)
                        v3 = lambda tl: View(tl.t[:, 0:NS * NS].rearrange("p (a b) -> p a b", b=NS), tl.bufs)
                        kb.I("dve", "tensor_tensor", v3(t1), hb(1), lb(1), ALU.mult)
                        kb.I("dve", "tensor_tensor", v3(Rre), hb(0), lb(0), ALU.mult)
                        kb.I("dve", "tensor_tensor", v3(Rre), v3(Rre), v3(t1), ALU.subtract)
                        kb.I("pool", "tensor_tensor", v3(t2), hb(0), lb(1), ALU.mult)
                        kb.I("pool", "tensor_tensor", v3(Rim), hb(1), lb(0), ALU.mult)
                        kb.I("pool", "tensor_tensor", v3(Rim), v3(Rim), v3(t2), ALU.add)
                        if "r_dbg" in dbg and uc == 0 and ccl == 0:
                            kb.dma("sp", rdb.v(rdb.ap[d, 0]), Rre.v(), adds=True)
                            kb.dma("sp", rdb.v(rdb.ap[d, 1]), Rim.v(), adds=True)

                        def rv(tl, a0, a1):
                            return tl.v(S_[:, a0:a1])
                        for b0 in range(0, NT, 512):
                            bw = min(512, NT - b0)
                            pr, pi_ = psum[4 + 2 * (it % 2)], psum[5 + 2 * (it % 2)]
                            br_, bi_ = bre[it % 2], bim[it % 2]
                            it += 1
                            kb.I("pe", "matmul", pr.v(S_[:, 0:bw]), Bpad.v(S_[:, ccl, 0, :]), useq.v(S_[:, uc, off + b0:off + b0 + bw], uc), start=True, stop=True)
                            kb.I("pe", "matmul", pi_.v(S_[:, 0:bw]), Bpad.v(S_[:, ccl, 1, :]), useq.v(S_[:, uc, off + b0:off + b0 + bw], uc), start=True, stop=True)
                            kb.I("act", "activation", br_.v(S_[:, 0:bw]), pr.v(S_[:, 0:bw]), AF.Copy)
                            kb.I("act", "activation", bi_.v(S_[:, 0:bw]), pi_.v(S_[:, 0:bw]), AF.Copy)
                            sl = S_[:, b0:b0 + bw]
                            kb.I("dve", "tensor_tensor", t1.v(sl), rv(Rim, b0, b0 + bw), bi_.v(S_[:, 0:bw]), ALU.mult)
                            kb.I("dve", "tensor_tensor", Zre.v(sl), rv(Rre, b0, b0 + bw), br_.v(S_[:, 0:bw]), ALU.mult)
                            kb.I("dve", "tensor_tensor", Zre.v(sl), Zre.v(sl), t1.v(sl), ALU.subtract)
                            kb.I("pool", "tensor_tensor", t2.v(sl), rv(Rim, b0, b0 + bw), br_.v(S_[:, 0:bw]), ALU.mult)
                            kb.I("pool", "tensor_tensor", Zim.v(sl), rv(Rre, b0, b0 + bw), bi_.v(S_[:, 0:bw]), ALU.mult)
                            kb.I("pool", "tensor_tensor", Zim.v(sl), Zim.v(sl), t2.v(sl), ALU.add)
                        rb = View(rr.t[:, col:col + 1].to_broadcast([128, NT]), rr.bufs)
                        if d == 0:
                            kb.I("dve", "tensor_tensor_scan", Sre.v(), rb, Zre.v(), 0.0, ALU.mult, ALU.add)
                            kb.I("dve", "tensor_tensor_scan", Sim.v(), rb, Zim.v(), 0.0, ALU.mult, ALU.add)
                        else:
                            kb.I("dve", "tensor_tensor_scan", View(Sre.t[:, ::-1], Sre.bufs), rb, View(Zre.t[:, ::-1], Zre.bufs), 0.0, ALU.mult, ALU.add)
                            kb.I("dve", "tensor_tensor_scan", View(Sim.t[:, ::-1], Sim.bufs), rb, View(Zim.t[:, ::-1], Zim.bufs), 0.0, ALU.mult, ALU.add)
                        sl = S_[:, lat0:lat0 + SEQ]
                        kb.I("dve", "tensor_tensor", Zre.v(sl), Rre.v(sl), Sre.v(sl), ALU.mult)
                        kb.I("pool", "tensor_tensor", Zim.v(sl), Rre.v(sl), Sim.v(sl), ALU.mult)
                        kb.I("pool", "tensor_tensor", Rre.v(sl), Rim.v(sl), Sre.v(sl), ALU.mult)
                        kb.I("dve", "tensor_tensor", Rim.v(sl), Rim.v(sl), Sim.v(sl), ALU.mult)
                        kb.I("dve", "tensor_tensor", Zre.v(sl), Zre.v(sl), Rim.v(sl), ALU.add)
                        kb.I("pool", "tensor_tensor", Zim.v(sl), Zim.v(sl), Rre.v(sl), ALU.subtract)
                        ce_r, ce_i = CeR[d], CeI[d]
                        kb.I("dve", "tensor_scalar", ct.v(), Cpad.v(S_[:, ccl, 1, :]), c1(ci), None, ALU.mult)
                        kb.I("dve", "scalar_tensor_tensor", ce_r.v(), Cpad.v(S_[:, ccl, 0, :]), c1(cr), ct.v(), ALU.mult, ALU.subtract)
                        kb.I("dve", "tensor_scalar", ct.v(), Cpad.v(S_[:, ccl, 1, :]), c1(cr), None, ALU.mult)
                        kb.I("dve", "scalar_tensor_tensor", ce_i.v(), Cpad.v(S_[:, ccl, 0, :]), c1(nci), ct.v(), ALU.mult, ALU.subtract)
                        first = (ccl == 0 and d == 0)
                        last = (ccl == 3 and d == 1)
                        for blk in range(4):
                            s0 = lat0 + blk * 512
                            kb.I("pe", "matmul", yps[blk].v(), ce_r.v(), Zre.v(S_[:, s0:s0 + 512]), start=first, stop=False)
                            kb.I("pe", "matmul", yps[blk].v(), ce_i.v(), Zim.v(S_[:, s0:s0 + 512]), start=False, stop=last)
                for blk in range(4):
                    s0 = blk * 512
                    kb.I("dve", "scalar_tensor_tensor", ysb.v(S_[:, s0:s0 + 512]), useq.v(S_[:, uc, CTX + s0:CTX + s0 + 512], uc),
                         dcol.v(S_[:, uc:uc + 1]), yps[blk].v(), ALU.mult, ALU.add)
                L = S_[:, 0:SEQ]
                kb.I("dve", "tensor_tensor", gt1.v(L), ysb.v(L), ysb.v(L), ALU.mult)
                kb.I("dve", "tensor_scalar", gt1.v(L), gt1.v(L), 0.044715, 1.0, ALU.mult, ALU.add)
                kb.I("dve", "tensor_tensor", gt1.v(L), gt1.v(L), ysb.v(L), ALU.mult)
                kb.I("act", "activation", gt1.v(L), gt1.v(L), AF.Sigmoid, scale=float(2.0 * np.sqrt(2.0 / np.pi)))
                kb.I("dve", "tensor_tensor", View(ge.t[:, uc, :].rearrange("p (r c) -> p c r", c=64), [ge.bufs[uc]]),
                     View(ysb.t[:, 0:SEQ].rearrange("p (c r) -> p c r", r=32), ysb.bufs),
                     View(gt1.t[:, 0:SEQ].rearrange("p (c r) -> p c r", r=32), gt1.bufs), ALU.mult)
            if "s5_dbg" in dbg:
                gd = Dram(kb, "s5_dbg", [4, 128, SEQ], BF16, kind="ExternalOutput")
                kb.dma("sp", gd.v(gd.ap.rearrange("k p t -> p k t")), ge.v())
            sgl = [Tile(kb, f"sgl{i}", [128, 512], F32) for i in range(2)]
            yo = [Tile(kb, f"yo{i}", [128, 512], BF16) for i in range(2)]
            n = 0
            for blk in range(4):
                for oc in range(4):
                    pa, pb = psum[(2 * n) % 8], psum[(2 * n + 1) % 8]
                    for (pt, o) in ((pa, oc), (pb, oc + 4)):
                        for kc in range(4):
                            kb.I("pe", "matmul", pt.v(), gluw.v(S_[:, kc, o * 128:(o + 1) * 128]), ge.v(S_[:, kc, blk * 512:(blk + 1) * 512], kc),
                                 start=(kc == 0), stop=(kc == 3))
                    sg_ = sgl[n % 2]
                    kb.I("act", "activation", sg_.v(), pb.v(), AF.Sigmoid)
                    kb.I("dve", "tensor_tensor", yo[n % 2].v(), sg_.v(), pa.v(), ALU.mult)
                    kb.dma("sp", yT.v(yT.ap[12 + oc, :, blk * 512:(blk + 1) * 512], 12 + oc), yo[n % 2].v(), adds=True)
                    n += 1

    if STOP_AFTER >= 5:
        with kb.phase():
            wo = Tile(kb, "wo", [128, KC, D], BF16)
            for q4 in range(4):
                kb.dma("pool", wo.v(S_[:, q4 * 4:(q4 + 1) * 4, :]), woutP.v(woutP.ap[:, q4 * 4:(q4 + 1) * 4, :]), adds=True)
            yb = [Tile(kb, f"yb{i}", [128, KC, 512], BF16) for i in range(2)]
            xr_ = [Tile(kb, f"xrF{i}", [128, 512], F32) for i in range(2)]
            xo_ = [Tile(kb, f"xoF{i}", [128, 512], F32) for i in range(2)]
            for blk in range(4):
                t0 = blk * 512
                ybt = yb[blk % 2]
                kb.dma("sp", ybt.v(), yT.v(yT.ap[:, :, t0:t0 + 512].rearrange("k p t -> p k t")))
                for dc in range(KC):
                    po = psum[dc % 4]
                    xr = xr_[dc % 2]
                    kb.dma("sp", xr.v(), x1T.v(x1T.ap[dc, :, t0:t0 + 512]))
                    for ic in range(KC):
                        kb.I("pe", "matmul", po.v(), wo.v(S_[:, ic, dc * 128:(dc + 1) * 128]), ybt.v(S_[:, ic, :]),
                             start=(ic == 0), stop=(ic == KC - 1))
                    o = xo_[dc % 2]
                    kb.I("dve", "scalar_tensor_tensor", o.v(), po.v(), gate.v(S_[:, 1, dc, 0:1]), xr.v(), ALU.mult, ALU.add)
                    kb.dma("sp", x2T.v(x2T.ap[dc, :, t0:t0 + 512], blk), o.v(), adds=True)

    if STOP_AFTER >= 6:
        with kb.phase():
            ffn_phase(up2P, dn2P, x2T, x3T, 2, blocks[:4])

    if STOP_AFTER >= 7:
        with kb.phase():
            nb = NormBufs()
            fo = Tile(kb, "fo", [128, KC, TB], F32, nbuf=KC)
            for bi, (t0, tb, t) in enumerate(blocks[:4]):
                norm_mod(nb, x3T, t0, tb, 0, 0, lambda kc, a, b: fo.v(S_[:, kc, a:b], kc),
                         G=lambda kc: nrm.v(S_[:, 3, kc:kc + 1]), Sv=False)
                kb.dma("sp", outT.v(outT.ap[:, :, t0:t0 + tb].rearrange("k p t -> p k t")), fo.v(S_[:, :, 0:tb]), adds=True)

    kb.finish(x1T.bufs + outT.bufs + h2T.bufs + uS.bufs + yT.bufs + x2T.bufs + x3T.bufs)
    return nc, kb


STOP_AFTER = 99
NBLK = 99
OVERLAP = True
NHEADS = NH
DBG_H = 0


def host_layout(inputs, b):
    f = np.float32
    x = inputs["x"][b]
    ctx = inputs["ctx"][b]
    xcat = np.concatenate([x, ctx], axis=0)
    xT = np.ascontiguousarray(xcat.T).reshape(KC, 128, NT)
    cc = np.stack([inputs["c"][b], inputs["c_ctx"]], axis=1)
    cT = np.ascontiguousarray(cc.reshape(KC, 128, 2).transpose(1, 0, 2))
    return {"xT": xT.astype(f), "cT": cT.astype(f)}


_SHARED = {}


def host_shared(inputs):
    f = np.float32
    sh = {}
    wmod = inputs["w_mod"][0]
    sh["wmodP"] = np.ascontiguousarray(wmod.reshape(KC, 128, 36, 512).transpose(2, 1, 0, 3))
    sh["bmodT"] = np.ascontiguousarray(inputs["b_mod"][0].reshape(NMOD * KC, 128).T)
    norms = np.stack([inputs["norm_ffn1"][0], inputs["norm_mix"][0], inputs["norm_ffn2"][0], inputs["final_norm"]], 0)
    sh["normsT"] = np.ascontiguousarray(norms.reshape(4, KC, 128).transpose(2, 0, 1))
    for nm, k in (("up1P", "ffn1_up"), ("up2P", "ffn2_up")):
        w = inputs[k][0]
        g = w[:, :DFF].reshape(KC, 128, NJ, 128)
        u = w[:, DFF:].reshape(KC, 128, NJ, 128)
        gu = np.concatenate([g, u], axis=3)
        sh[nm] = np.ascontiguousarray(gu.transpose(2, 1, 0, 3))
    for nm, k in (("dn1P", "ffn1_down"), ("dn2P", "ffn2_down")):
        w = inputs[k][0]
        sh[nm] = np.ascontiguousarray(w.reshape(NJ, 128, KC, 128).transpose(2, 1, 0, 3))
    win = inputs["w_in"][0]
    wsm = np.concatenate([win[:, 6144:6192], win[:, 6192:6704]], axis=1)
    sh["winS"] = np.ascontiguousarray(wsm.reshape(KC, 128, 560).transpose(1, 0, 2))
    sh["dnab"] = np.ascontiguousarray(np.stack([inputs["dn_a_log"][0].reshape(24), inputs["dn_dt_bias"][0].reshape(24)], 1))
    blocks_ = []
    for h in range(NH):
        cols = [win[:, c * 1536 + h * 128:c * 1536 + (h + 1) * 128] for c in range(4)]
        blocks_.append(np.concatenate(cols, axis=1).reshape(KC, 128, 512).transpose(1, 0, 2))
    sh["winP"] = np.ascontiguousarray(np.stack(blocks_, 0))
    cw = inputs["dn_conv"][0].reshape(9, 36, 128)
    sh["convT"] = np.ascontiguousarray(cw.transpose(2, 1, 0))
    sh["dnnorm"] = np.ascontiguousarray(inputs["dn_norm"][0].reshape(128, 1))
    def chan(a):
        return a.reshape(2, 16, 2, 64).transpose(2, 3, 0, 1).reshape(128, 32)
    ldt = np.broadcast_to(inputs["s5_log_dt"][0][:, :, None], (2, 32, 64))
    sh["s5a"] = np.ascontiguousarray(np.stack([chan(inputs["s5_a_re"][0]), chan(inputs["s5_a_im"][0]), chan(ldt)], axis=1))
    sh["s5bT"] = np.ascontiguousarray(np.stack([inputs["s5_b_re"][0].transpose(0, 2, 1), inputs["s5_b_im"][0].transpose(0, 2, 1)], 0))
    sh["s5cT"] = np.ascontiguousarray(np.stack([inputs["s5_c_re"][0].transpose(0, 2, 1), inputs["s5_c_im"][0].transpose(0, 2, 1)], 0))
    sh["s5dT"] = np.ascontiguousarray(inputs["s5_d"][0].reshape(4, 128).T)
    sh["gluP"] = np.ascontiguousarray(inputs["s5_glu"][0].reshape(4, 128, 1024).transpose(1, 0, 2))
    sh["woutP"] = np.ascontiguousarray(inputs["w_out"][0].reshape(KC, 128, D).transpose(1, 0, 2))
    i = np.arange(128)
    ident = np.eye(128, dtype=f)
    Lincl = (i[:, None] <= i[None, :]).astype(f)
    Uincl = (i[:, None] >= i[None, :]).astype(f)
    Lstr = (i[:, None] > i[None, :]).astype(f)
    Ustr = (i[:, None] < i[None, :]).astype(f)
    ones = np.ones((128, 128), f)
    bd = ((i[:, None] // 32) == (i[None, :] // 32)).astype(f)
    sh["consts"] = np.ascontiguousarray(np.stack([ident, ones, Lincl, Uincl, Lstr, Ustr, bd, 1.0 - bd], axis=1))
    return {k: v.astype(f) for k, v in sh.items()}


def kernel(**inputs):
    inputs = {k: np.asarray(v) for k, v in inputs.items()}
    nc, kb = build_program()
    sh = host_shared(inputs)
    in_maps = []
    for b in range(8):
        mp = dict(sh)
        mp.update(host_layout(inputs, b))
        in_maps.append(mp)
    res = run_bass_kernel_spmd(nc, in_maps, core_ids=list(range(8)))
    out = np.stack([r["outT"].reshape(D, SEQ).T for r in res.results], axis=0)
    return np.ascontiguousarray(out.astype(np.float32))
```

```python
import numpy as np
import concourse.bass as bass
import concourse.mybir as mybir
from concourse.bass_utils import run_bass_kernel_spmd

F32 = mybir.dt.float32
BF16 = mybir.dt.bfloat16
F32R = mybir.dt.float32r
AF = mybir.ActivationFunctionType
ALU = mybir.AluOpType
S_ = np.s_

D = 2048
SEQ = 2048
CTX = 256
NT = SEQ + CTX
KC = D // 128
DFF = 5632
NJ = DFF // 128
NH = 12
EPS = 1e-6
NMOD = 9


class Sem:
    def __init__(self, nc, name):
        self.h = nc.alloc_semaphore(name)
        self.count = 0


class Buf:
    __slots__ = ("w", "r", "name", "excl")

    def __init__(self, name=""):
        self.w = {}
        self.r = {}
        self.name = name
        self.excl = False


class View:
    __slots__ = ("ap", "bufs")

    def __init__(self, ap, bufs):
        self.ap = ap
        self.bufs = bufs


class Tile:
    def __init__(self, kb, name, shape, dtype, nbuf=1, space="sbuf"):
        nc = kb.nc
        kb.uid += 1
        name = f"{name}_{kb.uid}"
        if space == "sbuf":
            if kb.stack is not None:
                self.t = kb.stack.enter_context(nc.sbuf_tensor(name, list(shape), dtype))
            else:
                self.t = nc.alloc_sbuf_tensor(name, list(shape), dtype)
        elif space == "psum":
            self.t = nc.alloc_psum_tensor(name, list(shape), dtype)
        self.bufs = [Buf(f"{name}.{i}") for i in range(nbuf)]
        if space == "psum":
            for b in self.bufs:
                b.excl = True
        self.shape = shape

    def v(self, idx=None, bufs=None):
        ap = self.t[idx] if idx is not None else self.t[:]
        if bufs is None:
            bl = self.bufs
        elif isinstance(bufs, int):
            bl = [self.bufs[bufs]]
        else:
            bl = [self.bufs[i] for i in bufs]
        return View(ap, bl)


class Dram:
    def __init__(self, kb, name, shape, dtype, kind="Internal", nbuf=1):
        self.t = kb.nc.dram_tensor(name, list(shape), dtype, kind=kind)
        self.ap = self.t.ap()
        self.bufs = [Buf(f"{name}.{i}") for i in range(nbuf)]

    def v(self, ap, bufs=None):
        if bufs is None:
            bl = self.bufs
        elif isinstance(bufs, int):
            bl = [self.bufs[bufs]]
        else:
            bl = [self.bufs[i] for i in bufs]
        return View(ap, bl)


class Eng:
    def __init__(self, nc, name, obj):
        self.name = name
        self.obj = obj
        self.sem = Sem(nc, "e_" + name)
        self.seen = {}


class KB:
    NDMA = 24

    def __init__(self, nc):
        self.nc = nc
        self.E = {
            "pe": Eng(nc, "pe", nc.tensor),
            "dve": Eng(nc, "dve", nc.vector),
            "act": Eng(nc, "act", nc.scalar),
            "pool": Eng(nc, "pool", nc.gpsimd),
            "sp": Eng(nc, "sp", nc.sync),
        }
        self.dsem_q = {"sp": [Sem(nc, f"d{i}") for i in range(self.NDMA)],
                       "pool": [Sem(nc, f"dsw{i}") for i in range(12)]}
        self.dsem_q["act"] = self.dsem_q["sp"]
        self.dsem = self.dsem_q["sp"] + self.dsem_q["pool"]
        self.dnext_q = {"sp": 0, "pool": 0}
        self.nins = 0
        self.stack = None
        self.uid = 0
        self.psum = None
        self.pbi = 0

    def phase(self):
        return Phase(self)

    def bank(self):
        b = self.psum[self.pbi % 8]
        self.pbi += 1
        return b

    def barrier(self):
        sems = [e.sem for e in self.E.values()] + self.dsem
        for E in self.E.values():
            for s in sems:
                if s is E.sem or s.count == 0:
                    continue
                if E.seen.get(s, 0) < s.count:
                    E.obj.wait_ge(s.h, s.count)
                    E.seen[s] = s.count
                    self.nins += 1

    def _need(self, E, reads, writes, adds):
        need = {}
        own = E.sem

        def req(s, v):
            if need.get(s, 0) < v:
                need[s] = v

        for b in reads:
            for s, v in b.w.items():
                req(s, v)
            if b.excl:
                for s, v in b.r.items():
                    if s is not own:
                        req(s, v)
        for b in writes:
            for s, v in b.w.items():
                if s is not own:
                    req(s, v)
            for s, v in b.r.items():
                if s is not own:
                    req(s, v)
        for b in adds:
            for s, v in b.r.items():
                if s is not own:
                    req(s, v)
        if E.name == "pe":
            need.pop(own, None)
        return need

    def _wait(self, E, need):
        for s, v in need.items():
            if E.seen.get(s, 0) < v:
                E.obj.wait_ge(s.h, v)
                E.seen[s] = v
                self.nins += 1

    def _mark(self, sem, val, reads, writes, adds):
        for b in reads:
            if b.r.get(sem, 0) < val:
                b.r[sem] = val
        for b in writes:
            b.w = {sem: val}
            b.r = {}
        for b in adds:
            b.w[sem] = val

    def I(self, eng, method, out, *args, adds=False, extra_reads=(), **kw):
        E = self.E[eng]
        reads = []
        for b in extra_reads:
            reads.extend(b.bufs if isinstance(b, View) else [b])
        cargs = []
        for a in args:
            if isinstance(a, View):
                reads.extend(a.bufs)
                cargs.append(a.ap)
            else:
                cargs.append(a)
        ckw = {}
        wl = list(out.bufs)
        for k, a in kw.items():
            if isinstance(a, View):
                if k == "accum_out":
                    wl.extend(a.bufs)
                else:
                    reads.extend(a.bufs)
                ckw[k] = a.ap
            else:
                ckw[k] = a
        writes = [] if adds else wl
        addl = wl if adds else []
        need = self._need(E, reads, writes, addl)
        self._wait(E, need)
        ins = getattr(E.obj, method)(out.ap, *cargs, **ckw)
        E.sem.count += 1
        ins.then_inc(E.sem.h, 1)
        self._mark(E.sem, E.sem.count, reads, writes, addl)
        self.nins += 1
        return ins

    def dma(self, q, out, in_, adds=False):
        E = self.E[q]
        qk = "pool" if q == "pool" else "sp"
        pool_ = self.dsem_q[qk]
        slot = pool_[self.dnext_q[qk]]
        self.dnext_q[qk] = (self.dnext_q[qk] + 1) % len(pool_)
        reads = list(in_.bufs)
        wl = list(out.bufs)
        writes = [] if adds else wl
        addl = wl if adds else []
        need = {}
        own = None

        def req(s, v):
            if need.get(s, 0) < v:
                need[s] = v

        for b in reads:
            for s, v in b.w.items():
                req(s, v)
        for b in writes:
            for s, v in b.w.items():
                req(s, v)
            for s, v in b.r.items():
                req(s, v)
        esems = {e.sem for e in self.E.values()}
        for b in addl:
            for s, v in b.r.items():
                req(s, v)
            for s, v in b.w.items():
                if s in esems:
                    req(s, v)
        if slot.count:
            req(slot, slot.count)
        self._wait(E, need)
        ins = E.obj.dma_start(out=out.ap, in_=in_.ap)
        slot.count += 16
        ins.then_inc(slot.h, 16)
        self._mark(slot, slot.count, reads, writes, addl)
        self.nins += 1
        return ins

    def finish(self, bufs):
        E = self.E["sp"]
        need = {}
        for b in bufs:
            for s, v in b.w.items():
                if need.get(s, 0) < v:
                    need[s] = v
        for s, v in need.items():
            E.obj.wait_ge(s.h, v)


class Phase:
    def __init__(self, kb):
        self.kb = kb

    def __enter__(self):
        from contextlib import ExitStack
        self.prev = self.kb.stack
        self.stack = ExitStack()
        self.kb.stack = self.stack
        return self

    def __exit__(self, et, ev, tb):
        if et is None:
            self.kb.barrier()
            self.stack.close()
        self.kb.stack = self.prev
        return False


def tok_of_tile(tt):
    return tt * 128


def f32v(v):
    return View(v.ap.bitcast(F32), v.bufs)


def delta_pre(kb, w, q, bank, d, h, tt, KQ, ktok, vtok, tabs, cm):
    Gam, eG, bEG, kds, glast, btab = tabs
    ident, Lincl, Uincl, Lstr, Ustr, BD, NBD = cm
    col = d * 12 + h
    gcol = Gam.v(S_[:, tt, col:col + 1])
    bcol = btab.v(S_[:, tt, col:col + 1])
    kTc = KQ.v(S_[:, tt, 0:128])
    qTc = KQ.v(S_[:, tt, 128:256])
    is_lat = tt < 16
    maskS = Lstr if d == 0 else Ustr
    maskI = Lincl if d == 0 else Uincl
    I = kb.I
    pA = bank
    I("pe", "matmul", pA.v(S_[:, 0:128]), View(Gam.t[:, tt, col:col + 1].to_broadcast([128, 128]), Gam.bufs), ident, start=True, stop=True)
    if is_lat:
        I("pe", "matmul", pA.v(S_[:, 128:384]), kTc, KQ.v(S_[:, tt, 0:256]), start=True, stop=True)
    else:
        I("pe", "matmul", pA.v(S_[:, 128:256]), kTc, kTc, start=True, stop=True)
    yield
    I("dve", "tensor_scalar", w.F.v(), pA.v(S_[:, 0:128]), gcol, 0.0, ALU.subtract, ALU.max)
    if is_lat:
        I("dve", "tensor_scalar", w.E2.v(), pA.v(S_[:, 0:128]), gcol, 0.0, ALU.subtract, ALU.min)
    yield
    I("act", "activation", w.F.v(), w.F.v(), AF.Exp, scale=-1.0)
    if is_lat:
        I("act", "activation", w.E2.v(), w.E2.v(), AF.Exp)
        I("act", "activation", w.EB.v(), pA.v(S_[:, 0:128]), AF.Exp)
    I("act", "activation", w.RW.v(S_[:, 0:128]), ktok.v(S_[:, tt, :]), AF.Copy, scale=bEG.v(S_[:, tt, col:col + 1]))
    I("act", "activation", w.RW.v(S_[:, 128:256]), vtok.v(S_[:, tt, :]), AF.Copy, scale=bcol)
    I("act", "activation", q.kd.v(), ktok.v(S_[:, tt, :]), AF.Copy, scale=kds.v(S_[:, tt, col:col + 1]))
    yield
    I("pool", "tensor_tensor", w.M1.v(), w.F.v(), maskS, ALU.mult)
    if is_lat:
        I("pool", "tensor_tensor", w.E2.v(), w.E2.v(), maskI, ALU.mult)
        I("pool", "tensor_tensor", q.qdT.v(), f32v(qTc), w.EB.v(), ALU.mult)
    yield
    A0 = w.A[0]
    I("dve", "scalar_tensor_tensor", A0.v(), pA.v(S_[:, 128:256]), bcol, w.M1.v(), ALU.mult, ALU.mult)
    if is_lat:
        I("dve", "tensor_tensor", q.qkT.v(), pA.v(S_[:, 256:384]), w.E2.v(), ALU.mult)
    yield
    I("dve", "tensor_tensor", w.AL.v(), A0.v(), NBD, ALU.mult)
    I("dve", "tensor_tensor", A0.v(), A0.v(), BD, ALU.mult)
    yield
    pT = bank
    I("pe", "transpose", pT.v(S_[:, 0:128]), f32v(A0.v()), ident)
    yield
    BP0 = w.BP[0]
    I("act", "activation", BP0.v(S_[:, 0:128]), pT.v(S_[:, 0:128]), AF.Copy)
    yield
    I("dve", "tensor_tensor", BP0.v(S_[:, 128:256]), ident, BP0.v(S_[:, 0:128]), ALU.subtract)
    cur = 0
    for e in (1, 2, 4, 8):
        Ac, An = w.A[cur], w.A[1 - cur]
        BPc, BPn = w.BP[cur], w.BP[1 - cur]
        p1 = bank
        I("pe", "matmul", p1.v(S_[:, 0:128]), BPc.v(S_[:, 0:128]), Ac.v(), start=True, stop=True)
        if e == 1:
            I("pe", "matmul", p1.v(S_[:, 128:256]), Ac.v(), BPc.v(S_[:, 0:128]), start=True, stop=True)
        else:
            I("pe", "matmul", p1.v(S_[:, 128:384]), Ac.v(), BPc.v(), start=True, stop=True)
        yield
        I("act", "activation", An.v(), p1.v(S_[:, 0:128]), AF.Copy)
        I("act", "activation", BPn.v(S_[:, 0:128]), p1.v(S_[:, 128:256]), AF.Copy)
        if e == 1:
            I("act", "activation", BPn.v(S_[:, 128:256]), BPc.v(S_[:, 128:256]), AF.Copy)
        else:
            I("dve", "tensor_tensor", BPn.v(S_[:, 128:256]), p1.v(S_[:, 256:384]), BPc.v(S_[:, 128:256]), ALU.add)
        yield
        cur = 1 - cur
    Ac, BPc, BPn = w.A[cur], w.BP[cur], w.BP[1 - cur]
    p2 = bank
    I("pe", "matmul", p2.v(S_[:, 0:128]), Ac.v(), BPc.v(S_[:, 128:256]), start=True, stop=True)
    yield
    Q = BPn.v(S_[:, 128:256])
    I("dve", "tensor_tensor", Q, p2.v(S_[:, 0:128]), BPc.v(S_[:, 128:256]), ALU.add)
    yield
    p3 = bank
    I("pe", "matmul", p3.v(S_[:, 0:128]), w.AL.v(), Q, start=True, stop=True)
    I("pe", "matmul", p3.v(S_[:, 128:384]), Q, w.RW.v(), start=True, stop=True)
    yield
    I("act", "activation", w.M.v(), p3.v(S_[:, 0:128]), AF.Copy)
    I("act", "activation", w.V1.v(), p3.v(S_[:, 128:384]), AF.Copy)
    yield
    p3a = bank
    I("pe", "matmul", p3a.v(S_[:, 0:256]), w.M.v(), w.V1.v(), start=True, stop=True)
    yield
    I("act", "activation", w.Va.v(), p3a.v(S_[:, 0:256]), AF.Copy)
    yield
    p3b = bank
    I("pe", "matmul", p3b.v(S_[:, 0:256]), w.M.v(), w.Va.v(), start=True, stop=True)
    yield
    I("dve", "tensor_tensor", w.V1.v(), p3b.v(S_[:, 0:256]), w.V1.v(), ALU.add)
    yield
    p3c = bank
    I("pe", "matmul", p3c.v(S_[:, 0:256]), w.M.v(), w.V1.v(), start=True, stop=True)
    yield
    I("dve", "tensor_tensor", q.WU.v(), w.V1.v(), p3c.v(S_[:, 0:256]), ALU.subtract)
    yield
    p3d = bank
    I("pe", "transpose", p3d.v(S_[:, 0:128]), q.WU.v(S_[:, 0:128]), ident)
    yield
    I("act", "activation", q.WT.v(), p3d.v(S_[:, 0:128]), AF.Copy)


def delta_seq(kb, q, bank, ds, d, h, tt, oT, tabs):
    Gam, eG, bEG, kds, glast, btab = tabs
    col = d * 12 + h
    t0 = tok_of_tile(tt)
    is_lat = tt < 16
    I = kb.I
    S = ds.S[ds.si]
    Sn = ds.S[1 - ds.si]
    ds.si = 1 - ds.si
    I("pe", "matmul", bank.v(S_[:, 0:128]), q.WT.v(), S.v(), start=True, stop=True)
    if is_lat:
        I("pe", "matmul", bank.v(S_[:, 128:256]), S.v(), q.qdT.v(), start=True, stop=True)
    yield
    I("dve", "tensor_tensor", q.Ui.v(), q.WU.v(S_[:, 128:256]), bank.v(S_[:, 0:128]), ALU.subtract)
    if is_lat:
        I("act", "activation", q.ob.v(), bank.v(S_[:, 128:256]), AF.Copy)
    yield
    I("pe", "matmul", bank.v(S_[:, 256:384]), q.kd.v(), q.Ui.v(), start=True, stop=True)
    if is_lat:
        I("pe", "matmul", bank.v(S_[:, 384:512]), q.Ui.v(), q.qkT.v(), start=True, stop=True)
    yield
    I("dve", "scalar_tensor_tensor", Sn.v(), S.v(), glast.v(S_[:, tt, col:col + 1]), bank.v(S_[:, 256:384]), ALU.mult, ALU.add)
    if is_lat:
        I("dve", "tensor_tensor", q.ob.v(), bank.v(S_[:, 384:512]), q.ob.v(), ALU.add)
        I("pool", "tensor_tensor", oT.v(S_[:, t0:t0 + 128]), oT.v(S_[:, t0:t0 + 128]), q.ob.v(), ALU.add)
    yield


def run_interleaved(gens):
    act = list(gens)
    while act:
        for g in list(act):
            try:
                next(g)
            except StopIteration:
                act.remove(g)


def build_program(debug=()):
    nc = bass.Bass("TRN2", target_bir_lowering=False)
    kb = KB(nc)
    dbg = set(debug)

    def din(name, shape):
        return Dram(kb, name, shape, F32, kind="ExternalInput")

    def dscr(name, shape, dtype=F32, nbuf=1):
        return Dram(kb, name, shape, dtype, kind=("ExternalOutput" if name in dbg else "Internal"), nbuf=nbuf)

    xT = din("xT", [KC, 128, NT])
    cT = din("cT", [128, KC, 2])
    wmodP = din("wmodP", [36, 128, KC, 512])
    bmodT = din("bmodT", [128, NMOD * KC])
    normsT = din("normsT", [128, 4, KC])
    up1P = din("up1P", [NJ, 128, KC, 256])
    dn1P = din("dn1P", [KC, 128, NJ, 128])
    up2P = din("up2P", [NJ, 128, KC, 256])
    dn2P = din("dn2P", [KC, 128, NJ, 128])
    consts = din("consts", [128, 8, 128])
    outT = Dram(kb, "outT", [KC, 128, SEQ], F32, kind="ExternalOutput", nbuf=1)

    x1T = dscr("x1T", [KC, 128, NT], nbuf=5)
    x3T = dscr("x3T", [KC, 128, SEQ], nbuf=4)
    h2T = dscr("h2T", [KC, 128, NT], BF16, nbuf=5)
    uS = dscr("uS", [4, 128, NT + CTX], nbuf=4)
    winS = din("winS", [128, KC, 560])
    winP = din("winP", [NH, 128, KC, 512])
    convT = din("convT", [128, 36, 9])
    dnnorm = din("dnnorm", [128, 1])
    yT = dscr("yT", [KC, 128, SEQ], BF16, nbuf=KC)
    x2T = dscr("x2T", [KC, 128, SEQ], nbuf=4)
    s5a = din("s5a", [128, 3, 32])
    s5bT = din("s5bT", [2, 32, 16, 64])
    s5cT = din("s5cT", [2, 32, 64, 16])
    s5dT = din("s5dT", [128, 4])
    gluP = din("gluP", [128, 4, 1024])
    woutP = din("woutP", [128, KC, D])
    if "dn_dbg" in dbg:
        dnd = Dram(kb, "dn_dbg", [4, 128, NT], F32, kind="ExternalOutput")
    dnab = din("dnab", [24, 2])

    cst = Tile(kb, "cst", [128, 8, 128], F32)
    kb.dma("sp", cst.v(), consts.v(consts.ap))
    ident = cst.v(S_[:, 0, :])
    ones_bf = Tile(kb, "ones_bf", [128, 128], BF16)
    kb.I("dve", "memset", ones_bf.v(), 1.0)
    epsc = Tile(kb, "epsc", [128, 1], F32)
    kb.I("dve", "memset", epsc.v(), EPS)

    psum = [Tile(kb, f"ps{i}", [128, 512], F32, space="psum") for i in range(8)]
    kb.psum = psum

    Gt = Tile(kb, "Gt", [128, 3, KC, 2], F32)
    St = Tile(kb, "St", [128, 3, KC, 2], F32)
    gate = Tile(kb, "gate", [128, 3, KC, 2], F32)
    nrm = Tile(kb, "nrm", [128, 4, KC], F32)
    kb.dma("sp", nrm.v(), normsT.v(normsT.ap))

    with kb.phase():
        sc0 = Tile(kb, "sc0", [128, KC, 2], F32)
        sc = Tile(kb, "sc", [128, KC, 2], BF16)
        kb.dma("sp", sc0.v(), cT.v(cT.ap))
        kb.I("act", "activation", sc.v(), sc0.v(), AF.Silu)
        wm = [Tile(kb, f"wm{i}", [128, KC, 512], BF16) for i in range(3)]
        mps = psum[0]
        for P in range(36):
            w = wm[P % 3]
            kb.dma("pool", w.v(), wmodP.v(wmodP.ap[P]))
            for j4 in range(4):
                j = 4 * P + j4
                for kc in range(KC):
                    kb.I("pe", "matmul", mps.v(S_[:, 2 * j:2 * j + 2]),
                         w.v(S_[:, kc, j4 * 128:(j4 + 1) * 128]), sc.v(S_[:, kc, :]),
                         start=(kc == 0), stop=(kc == KC - 1))
        bm = Tile(kb, "bm", [128, NMOD * KC], F32)
        kb.dma("sp", bm.v(), bmodT.v(bmodT.ap))
        m = Tile(kb, "m", [128, NMOD * KC, 2], F32)
        kb.I("dve", "tensor_tensor", m.v(),
             View(mps.t[:, 0:288].rearrange("p (j t) -> p j t", t=2), mps.bufs),
             View(bm.t[:, :].unsqueeze(2).to_broadcast([128, NMOD * KC, 2]), bm.bufs), ALU.add)
        for j in range(3):
            r1 = (3 * j + 1) * KC
            kb.I("dve", "tensor_scalar", Gt.v(S_[:, j]), m.v(S_[:, r1:r1 + KC, :]), 1.0, None, ALU.add)
            kb.I("dve", "tensor_tensor", Gt.v(S_[:, j]), Gt.v(S_[:, j]),
                 View(nrm.t[:, j, :].unsqueeze(2).to_broadcast([128, KC, 2]), nrm.bufs), ALU.mult)
            r0 = (3 * j) * KC
            kb.I("dve", "tensor_copy", St.v(S_[:, j]), m.v(S_[:, r0:r0 + KC, :]))
            r2 = (3 * j + 2) * KC
            kb.I("dve", "tensor_scalar", gate.v(S_[:, j]), m.v(S_[:, r2:r2 + KC, :]),
                 (1.0 if j == 1 else 0.5), None, ALU.mult)
        if "m_dbg" in dbg:
            md = Dram(kb, "m_dbg", [128, NMOD * KC, 2], F32, kind="ExternalOutput")
            kb.dma("sp", md.v(md.ap), m.v())

    TB = 512

    TBN = 256

    class NormBufs:
        def __init__(self):
            self.xb = Tile(kb, "xb", [128, KC, TBN], F32)
            self.sq = Tile(kb, "sq", [128, KC, TBN], BF16)
            self.rstd = Tile(kb, "rstd", [128, TBN], F32)
            self.tmpn = [Tile(kb, f"tmpn{i}", [128, TBN], F32) for i in range(2)]

    def norm_mod(nb, src, t0, tb, j, t, dst, G=None, Sv=None):
        xb, sq, rstd, tmpn = nb.xb, nb.sq, nb.rstd, nb.tmpn
        for h0 in range(0, tb, TBN):
            hb = min(TBN, tb - h0)
            kb.dma("sp", xb.v(S_[:, :, 0:hb]), src.v(src.ap[:, :, t0 + h0:t0 + h0 + hb].rearrange("k p t -> p k t")))
            kb.I("act", "activation", sq.v(S_[:, :, 0:hb]), xb.v(S_[:, :, 0:hb]), AF.Square)
            pss = psum[7]
            for kc in range(KC):
                kb.I("pe", "matmul", pss.v(S_[:, 0:hb]), ones_bf.v(), sq.v(S_[:, kc, 0:hb]),
                     start=(kc == 0), stop=(kc == KC - 1))
            kb.I("act", "activation", rstd.v(S_[:, 0:hb]), pss.v(S_[:, 0:hb]), AF.Sqrt, bias=epsc.v(), scale=1.0 / D)
            kb.I("dve", "reciprocal", rstd.v(S_[:, 0:hb]), rstd.v(S_[:, 0:hb]))
            for kc in range(KC):
                tm = tmpn[kc % 2]
                gv = Gt.v(S_[:, j, kc, t:t + 1]) if G is None else G(kc)
                kb.I("dve", "scalar_tensor_tensor", tm.v(S_[:, 0:hb]), xb.v(S_[:, kc, 0:hb]),
                     gv, rstd.v(S_[:, 0:hb]), ALU.mult, ALU.mult)
                if Sv is None:
                    kb.I("act", "activation", dst(kc, h0, h0 + hb), tm.v(S_[:, 0:hb]), AF.Identity,
                         bias=St.v(S_[:, j, kc, t:t + 1]), scale=1.0)
                else:
                    kb.I("act", "activation", dst(kc, h0, h0 + hb), tm.v(S_[:, 0:hb]), AF.Copy)

    class FfnBufs:
        def __init__(self):
            self.hTs = [Tile(kb, f"hT{i}", [128, KC, TB], BF16, nbuf=KC) for i in range(2)]
            self.upt = [Tile(kb, f"upt{i}", [128, KC, 256], BF16) for i in range(3)]
            self.dnt = [Tile(kb, f"dnt{i}", [128, NJ, 128], BF16) for i in range(3)]
            self.actb = Tile(kb, "actb", [128, NJ, TB], BF16, nbuf=NJ)
            self.sg = [Tile(kb, f"sg{i}", [128, TB], F32) for i in range(2)]
            self.xres = [Tile(kb, f"xres{i}", [128, TB], F32) for i in range(2)]
            self.xo = [Tile(kb, f"xo{i}", [128, TB], F32) for i in range(2)]

    def ffn_up(fb, hT, upP, tb):
        upt, actb, sg = fb.upt, fb.actb, fb.sg

        def load_up(jj):
            kb.dma("pool", upt[jj % 3].v(), upP.v(upP.ap[jj]))
        load_up(0)
        load_up(1)
        for jj in range(NJ):
            if jj + 2 < NJ:
                load_up(jj + 2)
            w = upt[jj % 3]
            pg = psum[(2 * jj) % 4]
            pu = psum[(2 * jj + 1) % 4]
            for half, pt in ((0, pg), (1, pu)):
                for kc in range(KC):
                    kb.I("pe", "matmul", pt.v(S_[:, 0:tb]), w.v(S_[:, kc, half * 128:(half + 1) * 128]),
                         hT.v(S_[:, kc, 0:tb], kc), start=(kc == 0), stop=(kc == KC - 1))
            s_ = sg[jj % 2]
            kb.I("act", "activation", s_.v(S_[:, 0:tb]), pg.v(S_[:, 0:tb]), AF.Silu)
            kb.I("dve", "tensor_tensor", actb.v(S_[:, jj, 0:tb], jj), s_.v(S_[:, 0:tb]), pu.v(S_[:, 0:tb]), ALU.mult)

    def ffn_down(fb, dnP, src, dst, dst_buf, t0, tb, j, t):
        dnt, actb, xres, xo = fb.dnt, fb.actb, fb.xres, fb.xo

        def load_dn(dc):
            kb.dma("pool", dnt[dc % 3].v(), dnP.v(dnP.ap[dc]))
        load_dn(0)
        load_dn(1)
        for dc in range(KC):
            if dc + 2 < KC:
                load_dn(dc + 2)
            w = dnt[dc % 3]
            po = psum[4 + dc % 2]
            xr = xres[dc % 2]
            kb.dma("sp", xr.v(S_[:, 0:tb]), src.v(src.ap[dc, :, t0:t0 + tb]))
            for jj in range(NJ):
                kb.I("pe", "matmul", po.v(S_[:, 0:tb]), w.v(S_[:, jj, :]), actb.v(S_[:, jj, 0:tb], jj),
                     start=(jj == 0), stop=(jj == NJ - 1))
            o = xo[dc % 2]
            kb.I("dve", "scalar_tensor_tensor", o.v(S_[:, 0:tb]), po.v(S_[:, 0:tb]),
                 gate.v(S_[:, j, dc, t:t + 1]), xr.v(S_[:, 0:tb]), ALU.mult, ALU.add)
            kb.dma("sp", dst.v(dst.ap[dc, :, t0:t0 + tb], dst_buf), o.v(S_[:, 0:tb]), adds=True)

    def ffn_phase(upP, dnP, src, dst, j, blks):
        nb = NormBufs()
        fb = FfnBufs()
        hdst = lambda i: (lambda kc, a, b: fb.hTs[i % 2].v(S_[:, kc, a:b], kc))
        t0, tb, t = blks[0]
        norm_mod(nb, src, t0, tb, j, t, hdst(0))
        for bi, (t0, tb, t) in enumerate(blks):
            ffn_up(fb, fb.hTs[bi % 2], upP, tb)
            if bi + 1 < len(blks):
                n0, nbk, nt_ = blks[bi + 1]
                norm_mod(nb, src, n0, nbk, j, nt_, hdst(bi + 1))
            ffn_down(fb, dnP, src, dst, bi, t0, tb, j, t)

    blocks = [(i * TB, TB, 0) for i in range(SEQ // TB)] + [(SEQ, CTX, 1)]
    if STOP_AFTER >= 1:
        with kb.phase():
            ffn_phase(up1P, dn1P, xT, x1T, 0, blocks[:NBLK])

    NTT = NT // 128
    Gam = Tile(kb, "Gam", [128, NTT, 24], F32)
    eG = Tile(kb, "eG", [128, NTT, 24], F32)
    bEG = Tile(kb, "bEG", [128, NTT, 24], F32)
    kds = Tile(kb, "kds", [128, NTT, 24], F32)
    glast = Tile(kb, "glast", [128, NTT, 24], F32)
    btab = Tile(kb, "btab", [128, NTT, 24], F32)
    LinclV, UinclV = cst.v(S_[:, 2, :]), cst.v(S_[:, 3, :])
    LstrV, UstrV = cst.v(S_[:, 4, :]), cst.v(S_[:, 5, :])
    onesV = cst.v(S_[:, 1, :])
    if STOP_AFTER >= 2:
        with kb.phase():
            nb = NormBufs()
            h2 = Tile(kb, "h2", [128, KC, TB], BF16, nbuf=KC)
            wS = Tile(kb, "wS", [128, KC, 560], BF16)
            kb.dma("pool", wS.v(), winS.v(winS.ap))
            graw = Tile(kb, "graw", [24, NT], F32)
            braw = Tile(kb, "braw", [24, NT], F32)
            useq = Tile(kb, "useq", [128, 4, NT + CTX], F32, nbuf=4)
            ab = Tile(kb, "ab", [24, 2], F32)
            kb.dma("sp", ab.v(), dnab.v(dnab.ap))
            for bi, (t0, tb, t) in enumerate(blocks):
                norm_mod(nb, x1T, t0, tb, 1, t, lambda kc, a, b: h2.v(S_[:, kc, a:b], kc))
                kb.dma("sp", h2T.v(h2T.ap[:, :, t0:t0 + tb].rearrange("k p t -> p k t"), bi), h2.v(S_[:, :, 0:tb]), adds=True)
                pa, pb = psum[0], psum[1]
                for (pt, c0) in ((pa, 0), (pb, 24)):
                    for kc in range(KC):
                        kb.I("pe", "matmul", pt.v(S_[0:24, 0:tb]), wS.v(S_[:, kc, c0:c0 + 24]), h2.v(S_[:, kc, 0:tb], kc),
                             start=(kc == 0), stop=(kc == KC - 1))
                kb.I("act", "activation", graw.v(S_[:, t0:t0 + tb]), pa.v(S_[0:24, 0:tb]), AF.Copy)
                kb.I("dve", "tensor_copy", braw.v(S_[:, t0:t0 + tb]), pb.v(S_[0:24, 0:tb]))
                for uc in range(4):
                    pu = psum[2 + uc % 2]
                    for kc in range(KC):
                        kb.I("pe", "matmul", pu.v(S_[:, 0:tb]), wS.v(S_[:, kc, 48 + uc * 128:48 + (uc + 1) * 128]),
                             h2.v(S_[:, kc, 0:tb], kc), start=(kc == 0), stop=(kc == KC - 1))
                    if t == 1:
                        kb.I("act", "activation", useq.v(S_[:, uc, 0:CTX], uc), pu.v(S_[:, 0:tb]), AF.Copy)
                        kb.I("act", "activation", useq.v(S_[:, uc, NT:NT + CTX], uc), pu.v(S_[:, 0:tb]), AF.Copy)
                    else:
                        r0 = t0 // 64
                        nr = tb // 64
                        dst = useq.t[:, uc, CTX:NT].rearrange("p (c r) -> p r c", r=32)[:, r0:r0 + nr, :]
                        src = pu.t[:, 0:tb].rearrange("p (r c) -> p r c", c=64)
                        kb.I("act", "activation", View(dst, [useq.bufs[uc]]), View(src, pu.bufs), AF.Copy)
            for uc in range(4):
                kb.dma("sp", uS.v(uS.ap[uc], uc), useq.v(S_[:, uc, :], uc))
            nea = Tile(kb, "nea", [24, 1], F32)
            kb.I("act", "activation", nea.v(), ab.v(S_[:, 0:1]), AF.Exp)
            kb.I("dve", "tensor_scalar", nea.v(), nea.v(), -1.0, None, ALU.mult)
            kb.I("act", "activation", graw.v(), graw.v(), AF.Exp, bias=ab.v(S_[:, 1:2]), scale=1.0)
            kb.I("act", "activation", graw.v(), graw.v(), AF.Ln, bias=1.0, scale=1.0)
            kb.I("dve", "tensor_scalar", graw.v(), graw.v(), nea.v(S_[:, 0:1]), None, ALU.mult)
            kb.I("act", "activation", braw.v(), braw.v(), AF.Sigmoid)
            gtok = Tile(kb, "gtok", [128, NTT, 24], F32)
            tot = Tile(kb, "tot", [128, NTT, 24], F32)
            for tt in range(NTT):
                pg, pbb = psum[0], psum[1]
                kb.I("pe", "transpose", pg.v(S_[:, 0:24]), graw.v(S_[:, tt * 128:(tt + 1) * 128]), cst.v(S_[0:24, 0, 0:24]))
                kb.I("pe", "transpose", pbb.v(S_[:, 0:24]), braw.v(S_[:, tt * 128:(tt + 1) * 128]), cst.v(S_[0:24, 0, 0:24]))
                kb.I("act", "activation", gtok.v(S_[:, tt, :]), pg.v(S_[:, 0:24]), AF.Copy)
                kb.I("dve", "tensor_copy", btab.v(S_[:, tt, :]), pbb.v(S_[:, 0:24]))
                pc, pd = psum[2], psum[3]
                kb.I("pe", "matmul", pc.v(S_[:, 0:12]), LinclV, gtok.v(S_[:, tt, 0:12]), start=True, stop=True)
                kb.I("pe", "matmul", pc.v(S_[:, 12:24]), UinclV, gtok.v(S_[:, tt, 12:24]), start=True, stop=True)
                kb.I("pe", "matmul", pd.v(S_[:, 0:24]), onesV, gtok.v(S_[:, tt, :]), start=True, stop=True)
                kb.I("act", "activation", Gam.v(S_[:, tt, :]), pc.v(S_[:, 0:24]), AF.Copy)
                kb.I("dve", "tensor_copy", tot.v(S_[:, tt, :]), pd.v(S_[:, 0:24]))
            kb.I("act", "activation", eG.v(), Gam.v(), AF.Exp)
            kb.I("dve", "tensor_tensor", bEG.v(), eG.v(), btab.v(), ALU.mult)
            kb.I("dve", "tensor_tensor", kds.v(), tot.v(), Gam.v(), ALU.subtract)
            kb.I("act", "activation", kds.v(), kds.v(), AF.Exp)
            kb.I("act", "activation", glast.v(), tot.v(), AF.Exp)
            if "tab_dbg" in dbg:
                td = Dram(kb, "tab_dbg", [6, 128, NTT, 24], F32, kind="ExternalOutput")
                for i, tl in enumerate([Gam, eG, bEG, kds, glast, btab]):
                    kb.dma("sp", td.v(td.ap[i]), tl.v(), adds=True)

    GB = 3
    if STOP_AFTER >= 3:
        with kb.phase():
            PADL = 65
            XW = PADL + SEQ + PADL
            vT = Tile(kb, "vT", [128, NT], F32)
            KQ = Tile(kb, "KQ", [128, NTT, 256], F32R)
            zT = Tile(kb, "zT", [128, SEQ], F32)
            ktok = Tile(kb, "ktok", [128, NTT, 128], F32)
            vtok = Tile(kb, "vtok", [128, NTT, 128], F32)
            oT = Tile(kb, "oT", [128, SEQ], F32)
            ybf = Tile(kb, "ybf", [128, SEQ], BF16)
            convw = Tile(kb, "convw", [128, 36, 9], F32)
            kb.dma("sp", convw.v(), convT.v(convT.ap))
            dnw = Tile(kb, "dnw", [128, 1], F32)
            kb.dma("sp", dnw.v(), dnnorm.v(dnnorm.ap))
            rinv = Tile(kb, "rinv", [128, 512], F32)
            tabs = (Gam, eG, bEG, kds, glast, btab)
            cmv = (ident, LinclV, UinclV, LstrV, UstrV, cst.v(S_[:, 6, :]), cst.v(S_[:, 7, :]))

            class WS:
                def __init__(self, n):
                    mk = lambda nm, w=128, dt=F32: Tile(kb, f"{nm}{n}", [128, w], dt)
                    self.F, self.EB, self.M1, self.E2 = mk("F"), mk("EB"), mk("M1"), mk("E2")
                    self.A = [mk("A0", 128, F32R), mk("A1", 128, F32R)]
                    self.BP = [mk("BP0", 256, F32R), mk("BP1", 256, F32R)]
                    self.RW, self.V1, self.Va = mk("RW", 256, F32R), mk("V1", 256, F32R), mk("Va", 256, F32R)
                    self.AL, self.M = mk("AL", 128, F32R), mk("M", 128, F32R)

            class WQ:
                def __init__(self, n):
                    mk = lambda nm, w=128: Tile(kb, f"{nm}{n}", [128, w], F32)
                    self.qkT, self.qdT, self.kd, self.WT, self.Ui = mk("qkT"), mk("qdT"), mk("kd"), mk("WT"), mk("Ui")
                    self.ob = mk("ob")
                    self.WU = mk("WU", 256)

            class DS:
                def __init__(self, d):
                    self.S = [Tile(kb, f"S{d}_{i}", [128, 128], F32) for i in range(2)]
                    self.si = 0

            for h in range(NHEADS):
                with kb.phase():
                    winh = Tile(kb, "winh", [128, KC, 512], BF16)
                    h2b = [Tile(kb, f"h2b{i}", [128, KC, 256], BF16) for i in range(2)]
                    Xs = [[Tile(kb, f"X{n}{i}", [128, XW], F32) for n in ("0", "L", "R")] for i in range(2)]
                    C0 = [Tile(kb, f"C0{i}", [128, CTX + 2], F32) for i in range(3)]
                    sqs = Tile(kb, "sqs", [128, NT], F32)
                    qacc = Tile(kb, "qacc", [128, NT], F32)
                    kacc = Tile(kb, "kacc", [128, NT], F32)
                    qkvT = [qacc, kacc, vT]
                    kb.dma("pool", winh.v(), winP.v(winP.ap[h]))
                    for tl in Xs[0] + Xs[1] + C0:
                        kb.I("pool", "memset", tl.v(), 0.0)
                    nblk = NT // 256
                    nld = 0
                    for cpair in range(2):
                      for bi in range(nblk):
                        t0 = bi * 256
                        hb = h2b[nld % 2]
                        nld += 1
                        kb.dma("sp", hb.v(), h2T.v(h2T.ap[:, :, t0:t0 + 256].rearrange("k p t -> p k t")))
                        for comp in (2 * cpair, 2 * cpair + 1):
                            x0, XL, XR = Xs[comp % 2]
                            if comp == 3 and t0 >= SEQ:
                                continue
                            pt = kb.bank()
                            for kc in range(KC):
                                kb.I("pe", "matmul", pt.v(S_[:, 0:256]), winh.v(S_[:, kc, comp * 128:(comp + 1) * 128]),
                                     hb.v(S_[:, kc, :]), start=(kc == 0), stop=(kc == KC - 1))
                            if comp == 3:
                                kb.I("act", "activation", zT.v(S_[:, t0:t0 + 256]), pt.v(S_[:, 0:256]), AF.Silu)
                            elif t0 < SEQ:
                                kb.I("act", "activation", x0.v(S_[:, PADL + t0:PADL + t0 + 256]), pt.v(S_[:, 0:256]), AF.Copy)
                                p3v = lambda c0, c1: View(pt.t[:, 0:256].rearrange("p (r c) -> p r c", c=64)[:, :, c0:c1], pt.bufs)
                                x3v = lambda tl, c0, c1: View(tl.t[:, PADL + t0:PADL + t0 + 256].rearrange("p (r c) -> p r c", c=64)[:, :, c0:c1], tl.bufs)
                                kb.I("act", "activation", x3v(XL, 0, 63), p3v(0, 63), AF.Copy)
                                kb.I("act", "activation", x3v(XR, 1, 64), p3v(1, 64), AF.Copy)
                            else:
                                kb.I("act", "activation", C0[comp].v(S_[:, 1:1 + CTX]), pt.v(S_[:, 0:256]), AF.Copy)
                      for comp in (2 * cpair, 2 * cpair + 1):
                        x0, XL, XR = Xs[comp % 2]
                        if comp == 3:
                            continue
                        acc = qkvT[comp]
                        ch = comp * 12 + h
                        srcs = (XL, x0, XR)
                        order = [(1, 1)] + [(i, j) for i in range(3) for j in range(3) if (i, j) != (1, 1)]
                        for n, (i, j) in enumerate(order):
                            off = PADL + (i - 1) * 64 + (j - 1)
                            sv = srcs[j].v(S_[:, off:off + SEQ])
                            wv = convw.v(S_[:, ch, i * 3 + j:i * 3 + j + 1])
                            if n == 0:
                                kb.I("dve", "tensor_scalar", acc.v(S_[:, 0:SEQ]), sv, wv, None, ALU.mult)
                            else:
                                kb.I("dve", "scalar_tensor_tensor", acc.v(S_[:, 0:SEQ]), sv, wv, acc.v(S_[:, 0:SEQ]), ALU.mult, ALU.add)
                        for n, j in enumerate((1, 0, 2)):
                            sv = C0[comp].v(S_[:, j:j + CTX])
                            wv = convw.v(S_[:, ch, 3 + j:3 + j + 1])
                            if n == 0:
                                kb.I("dve", "tensor_scalar", acc.v(S_[:, SEQ:NT]), sv, wv, None, ALU.mult)
                            else:
                                kb.I("dve", "scalar_tensor_tensor", acc.v(S_[:, SEQ:NT]), sv, wv, acc.v(S_[:, SEQ:NT]), ALU.mult, ALU.add)
                        kb.I("act", "activation", acc.v(), acc.v(), AF.Silu)
                        if comp < 2:
                            kb.I("act", "activation", sqs.v(), acc.v(), AF.Square)
                            for b0 in range(0, NT, 512):
                                bw = min(512, NT - b0)
                                pt = kb.bank()
                                kb.I("pe", "matmul", pt.v(S_[:, 0:bw]), onesV, sqs.v(S_[:, b0:b0 + bw]), start=True, stop=True)
                                kb.I("act", "activation", rinv.v(S_[:, 0:bw]), pt.v(S_[:, 0:bw]), AF.Sqrt, bias=epsc.v(), scale=1.0)
                                kb.I("dve", "reciprocal", rinv.v(S_[:, 0:bw]), rinv.v(S_[:, 0:bw]))
                                nt4 = bw // 128
                                tl0 = b0 // 128
                                v3_ = lambda ap: ap.rearrange("p (a b) -> p a b", b=128)
                                a3 = View(v3_(acc.t[:, b0:b0 + bw]), acc.bufs)
                                r3 = View(v3_(rinv.t[:, 0:bw]), rinv.bufs)
                                if comp == 0:
                                    kb.I("dve", "scalar_tensor_tensor", KQ.v(S_[:, tl0:tl0 + nt4, 128:256]), a3,
                                         128.0 ** -0.5, r3, ALU.mult, ALU.mult)
                                else:
                                    kb.I("dve", "tensor_tensor", KQ.v(S_[:, tl0:tl0 + nt4, 0:128]), a3, r3, ALU.mult)
                    if "dn_dbg" in dbg and h == DBG_H:
                        kb.dma("sp", dnd.v(dnd.ap[2]), vT.v(), adds=True)
                    for (which, dstT) in ((0, ktok), (1, vtok)):
                        for t4 in range(0, NTT, 4):
                            n4 = min(4, NTT - t4)
                            pt = kb.bank()
                            for q4 in range(n4):
                                tt = t4 + q4
                                tok0 = tok_of_tile(tt)
                                srcv = f32v(KQ.v(S_[:, tt, 0:128])) if which == 0 else vT.v(S_[:, tok0:tok0 + 128])
                                kb.I("pe", "transpose", pt.v(S_[:, q4 * 128:(q4 + 1) * 128]), srcv, ident)
                            kb.I("act", "activation", dstT.v(S_[:, t4:t4 + n4, :]),
                                 View(pt.t[:, 0:n4 * 128].rearrange("p (a b) -> p a b", b=128), pt.bufs), AF.Copy)
                with kb.phase():
                    ws = [[WS(f"_{d}{g}") for g in range(GB)] for d in range(2)]
                    wq = [[[WQ(f"_{p}{d}{g}") for g in range(GB)] for d in range(2)] for p in range(2)]
                    dss = [DS(0), DS(1)]
                    sq2 = Tile(kb, "sq2", [128, SEQ], F32)
                    kb.I("pool", "memset", oT.v(), 0.0)
                    for d in range(2):
                        kb.I("pool", "memset", dss[d].S[0].v(), 0.0)
                    orders = [[16, 17] + list(range(16)), [17, 16] + list(range(15, -1, -1))]
                    nb = NTT // GB

                    def pre_gens(b):
                        gl = []
                        for g in range(GB):
                            for d in range(2):
                                gl.append(delta_pre(kb, ws[d][g], wq[b % 2][d][g], psum[g * 2 + d], d, h, orders[d][b * GB + g], KQ, ktok, vtok, tabs, cmv))
                        return gl

                    def seq_stream(b, d):
                        for g in range(GB):
                            yield from delta_seq(kb, wq[b % 2][d][g], psum[6 + d], dss[d], d, h, orders[d][b * GB + g], oT, tabs)
                    if OVERLAP:
                        run_interleaved(pre_gens(0))
                        for b in range(1, nb):
                            run_interleaved([seq_stream(b - 1, 0), seq_stream(b - 1, 1)] + pre_gens(b))
                        run_interleaved([seq_stream(nb - 1, 0), seq_stream(nb - 1, 1)])
                    else:
                        for b in range(nb):
                            run_interleaved(pre_gens(b))
                            run_interleaved([seq_stream(b, 0), seq_stream(b, 1)])
                    if "dn_dbg" in dbg and h == DBG_H:
                        kb.dma("sp", dnd.v(dnd.ap[3, :, 0:SEQ]), oT.v(), adds=True)
                    kb.I("act", "activation", sq2.v(), oT.v(), AF.Square)
                    for b0 in range(0, SEQ, 512):
                        pt = kb.bank()
                        kb.I("pe", "matmul", pt.v(), onesV, sq2.v(S_[:, b0:b0 + 512]), start=True, stop=True)
                        kb.I("act", "activation", rinv.v(), pt.v(), AF.Sqrt, bias=epsc.v(), scale=1.0 / 128)
                        kb.I("dve", "reciprocal", rinv.v(), rinv.v())
                        kb.I("dve", "tensor_tensor", oT.v(S_[:, b0:b0 + 512]), oT.v(S_[:, b0:b0 + 512]), rinv.v(), ALU.mult)
                        kb.I("dve", "scalar_tensor_tensor", ybf.v(S_[:, b0:b0 + 512]), oT.v(S_[:, b0:b0 + 512]), dnw.v(),
                             zT.v(S_[:, b0:b0 + 512]), ALU.mult, ALU.mult)
                    kb.dma("sp", yT.v(yT.ap[h], h), ybf.v())

    if STOP_AFTER >= 4:
        with kb.phase():
            useq = Tile(kb, "useq", [128, 4, NT + CTX], F32, nbuf=4)
            for uc in range(4):
                kb.dma("sp", useq.v(S_[:, uc, :], uc), uS.v(uS.ap[uc], uc))
            sp_ = Tile(kb, "s5p", [128, 3, 32], F32)
            kb.dma("sp", sp_.v(), s5a.v(s5a.ap))
            mk = lambda n: Tile(kb, n, [128, 32], F32)
            dtt, ar, th, rr, sn, cs, tmpa, tmpb, nr_, ni_, den, cr, ci, nci = [mk(f"s5s{i}") for i in range(14)]
            aRe, aIm, ldt = sp_.v(S_[:, 0, :]), sp_.v(S_[:, 1, :]), sp_.v(S_[:, 2, :])
            V = "dve"
            kb.I("act", "activation", dtt.v(), ldt, AF.Exp)
            kb.I(V, "tensor_tensor", ar.v(), aRe, dtt.v(), ALU.mult)
            kb.I(V, "tensor_tensor", th.v(), aIm, dtt.v(), ALU.mult)
            kb.I("act", "activation", rr.v(), ar.v(), AF.Exp)

            def sin_of(dst, src, shift):
                kb.I(V, "tensor_scalar", dst.v(), src.v(), float(shift), None, ALU.add)
                for _ in range(5):
                    kb.I(V, "tensor_scalar", tmpa.v(), dst.v(), float(np.pi), float(2 * np.pi), ALU.is_gt, ALU.mult)
                    kb.I(V, "tensor_tensor", dst.v(), dst.v(), tmpa.v(), ALU.subtract)
                kb.I("act", "activation", dst.v(), dst.v(), AF.Sin)
            sin_of(sn, th, 0.0)
            sin_of(cs, th, np.pi / 2)
            kb.I(V, "tensor_tensor", nr_.v(), rr.v(), cs.v(), ALU.mult)
            kb.I(V, "tensor_scalar", nr_.v(), nr_.v(), -1.0, None, ALU.add)
            kb.I(V, "tensor_tensor", ni_.v(), rr.v(), sn.v(), ALU.mult)
            kb.I(V, "tensor_tensor", den.v(), aRe, aRe, ALU.mult)
            kb.I(V, "tensor_tensor", tmpa.v(), aIm, aIm, ALU.mult)
            kb.I(V, "tensor_tensor", den.v(), den.v(), tmpa.v(), ALU.add)
            kb.I(V, "reciprocal", den.v(), den.v())
            kb.I(V, "tensor_tensor", cr.v(), nr_.v(), aRe, ALU.mult)
            kb.I(V, "tensor_tensor", tmpa.v(), ni_.v(), aIm, ALU.mult)
            kb.I(V, "tensor_tensor", cr.v(), cr.v(), tmpa.v(), ALU.add)
            kb.I(V, "tensor_tensor", cr.v(), cr.v(), den.v(), ALU.mult)
            kb.I(V, "tensor_tensor", ci.v(), ni_.v(), aRe, ALU.mult)
            kb.I(V, "tensor_tensor", tmpa.v(), nr_.v(), aIm, ALU.mult)
            kb.I(V, "tensor_tensor", ci.v(), ci.v(), tmpa.v(), ALU.subtract)
            kb.I(V, "tensor_tensor", ci.v(), ci.v(), den.v(), ALU.mult)
            kb.I(V, "tensor_scalar", nci.v(), ci.v(), -1.0, None, ALU.mult)
            NPW = 12
            wpow = Tile(kb, "wpow", [128, NPW, 2, 32], F32)
            kb.I(V, "tensor_copy", wpow.v(S_[:, 0, 0, :]), cs.v())
            kb.I(V, "tensor_scalar", wpow.v(S_[:, 0, 1, :]), sn.v(), -1.0, None, ALU.mult)
            for k in range(NPW - 1):
                x_, y_ = wpow.v(S_[:, k, 0, :]), wpow.v(S_[:, k, 1, :])
                kb.I(V, "tensor_tensor", tmpa.v(), x_, x_, ALU.mult)
                kb.I(V, "tensor_tensor", tmpb.v(), y_, y_, ALU.mult)
                kb.I(V, "tensor_tensor", wpow.v(S_[:, k + 1, 0, :]), tmpa.v(), tmpb.v(), ALU.subtract)
                kb.I(V, "tensor_tensor", tmpa.v(), x_, y_, ALU.mult)
                kb.I(V, "tensor_scalar", wpow.v(S_[:, k + 1, 1, :]), tmpa.v(), 2.0, None, ALU.mult)
            NS = 48
            LOf = Tile(kb, "LOf", [128, 2, 32, NS], F32)
            HIf = Tile(kb, "HIf", [128, 2, 32, NS], F32)
            with kb.phase():
                cmul_t = [Tile(kb, f"cmt{i}", [128, 32, NS], F32) for i in range(2)]
                Tt = Tile(kb, "Tt", [128, 2, 32, NS], F32)

                def cpow_table(T, wp):
                    kb.I(V, "memset", T.v(S_[:, 0, :, 0:1]), 1.0)
                    kb.I(V, "memset", T.v(S_[:, 1, :, 0:1]), 0.0)
                    for j in range(6):
                        n = 1 << j
                        seg = min(n, NS - n)
                        wr, wi = wp[j]
                        bc = lambda v: View(v.ap.unsqueeze(2).to_broadcast([128, 32, seg]), v.bufs)
                        a_re, a_im = T.v(S_[:, 0, :, 0:seg]), T.v(S_[:, 1, :, 0:seg])
                        ta, tb_ = cmul_t[0].v(S_[:, :, 0:seg]), cmul_t[1].v(S_[:, :, 0:seg])
                        kb.I(V, "tensor_tensor", ta, a_im, bc(wi), ALU.mult)
                        kb.I(V, "tensor_tensor", tb_, a_re, bc(wr), ALU.mult)
                        kb.I(V, "tensor_tensor", T.v(S_[:, 0, :, n:n + seg]), tb_, ta, ALU.subtract)
                        kb.I(V, "tensor_tensor", ta, a_im, bc(wr), ALU.mult)
                        kb.I(V, "tensor_tensor", tb_, a_re, bc(wi), ALU.mult)
                        kb.I(V, "tensor_tensor", T.v(S_[:, 1, :, n:n + seg]), tb_, ta, ALU.add)

                def finalize(dst):
                    for ri in range(2):
                        kb.I(V, "tensor_copy", dst.v(S_[:, ri, 0:16, :]), Tt.v(S_[:, ri, 0:16, :]))
                        kb.I(V, "tensor_copy", dst.v(S_[:, ri, 16:32, :]), View(Tt.t[:, ri, 16:32, ::-1], Tt.bufs))
                cpow_table(Tt, [(wpow.v(S_[:, j, 0, :]), wpow.v(S_[:, j, 1, :])) for j in range(6)])
                finalize(LOf)
                w48 = Tile(kb, "w48", [128, 6, 2, 32], F32)
                x5, y5, x4, y4 = (wpow.v(S_[:, 5, 0, :]), wpow.v(S_[:, 5, 1, :]), wpow.v(S_[:, 4, 0, :]), wpow.v(S_[:, 4, 1, :]))
                kb.I(V, "tensor_tensor", tmpa.v(), x5, x4, ALU.mult)
                kb.I(V, "tensor_tensor", tmpb.v(), y5, y4, ALU.mult)
                kb.I(V, "tensor_tensor", w48.v(S_[:, 0, 0, :]), tmpa.v(), tmpb.v(), ALU.subtract)
                kb.I(V, "tensor_tensor", tmpa.v(), x5, y4, ALU.mult)
                kb.I(V, "tensor_tensor", tmpb.v(), y5, x4, ALU.mult)
                kb.I(V, "tensor_tensor", w48.v(S_[:, 0, 1, :]), tmpa.v(), tmpb.v(), ALU.add)
                for k in range(5):
                    x_, y_ = w48.v(S_[:, k, 0, :]), w48.v(S_[:, k, 1, :])
                    kb.I(V, "tensor_tensor", tmpa.v(), x_, x_, ALU.mult)
                    kb.I(V, "tensor_tensor", tmpb.v(), y_, y_, ALU.mult)
                    kb.I(V, "tensor_tensor", w48.v(S_[:, k + 1, 0, :]), tmpa.v(), tmpb.v(), ALU.subtract)
                    kb.I(V, "tensor_tensor", tmpa.v(), x_, y_, ALU.mult)
                    kb.I(V, "tensor_scalar", w48.v(S_[:, k + 1, 1, :]), tmpa.v(), 2.0, None, ALU.mult)
                cpow_table(Tt, [(w48.v(S_[:, j, 0, :]), w48.v(S_[:, j, 1, :])) for j in range(6)])
                finalize(HIf)
            Bpad = Tile(kb, "Bpad", [128, 4, 2, 128], F32)
            Cpad = Tile(kb, "Cpad", [128, 4, 2, 128], F32)

            def load_pads(uc):
                kb.I("pool", "memset", Bpad.v(), 0.0)
                kb.I("pool", "memset", Cpad.v(), 0.0)
                for g in range(uc * 8, uc * 8 + 8):
                    cl, e, lg = (g // 2) % 4, g % 2, g % 8
                    for ri in range(2):
                        kb.dma("sp", Bpad.v(S_[lg * 16:(lg + 1) * 16, cl, ri, e * 64:(e + 1) * 64]), s5bT.v(s5bT.ap[ri, g]), adds=True)
                        kb.dma("sp", Cpad.v(S_[e * 64:(e + 1) * 64, cl, ri, lg * 16:(lg + 1) * 16]), s5cT.v(s5cT.ap[ri, g]), adds=True)
            dcol = Tile(kb, "dcol", [128, 4], F32)
            kb.dma("sp", dcol.v(), s5dT.v(s5dT.ap))
            gluw = Tile(kb, "gluw", [128, 4, 1024], BF16)
            kb.dma("pool", gluw.v(), gluP.v(gluP.ap))
            Rre, Rim = Tile(kb, "Rre", [128, NT], F32), Tile(kb, "Rim", [128, NT], F32)
            Zre, Zim = Tile(kb, "Zre", [128, NT], F32), Tile(kb, "Zim", [128, NT], F32)
            Sre, Sim = Tile(kb, "Sre", [128, NT], F32), Tile(kb, "Sim", [128, NT], F32)
            t1, t2 = Sre, Sim
            bre = [Tile(kb, f"bre{i}", [128, 512], F32) for i in range(2)]
            bim = [Tile(kb, f"bim{i}", [128, 512], F32) for i in range(2)]
            CeR = [Tile(kb, f"CeR{i}", [128, 128], F32) for i in range(2)]
            CeI = [Tile(kb, f"CeI{i}", [128, 128], F32) for i in range(2)]
            ct = Tile(kb, "s5ct", [128, 128], F32)
            ysb, gt1 = t2, t1
            ge = Tile(kb, "ge", [128, 4, SEQ], BF16, nbuf=4)
            it = 0
            if "r_dbg" in dbg:
                rdb = Dram(kb, "r_dbg", [2, 2, 128, NT], F32, kind="ExternalOutput")
                ldb = Dram(kb, "lo_dbg", [2, 128, 2, 32, NS], F32, kind="ExternalOutput")
                kb.dma("sp", ldb.v(ldb.ap[0]), LOf.v(), adds=True)
                kb.dma("sp", ldb.v(ldb.ap[1]), HIf.v(), adds=True)
            for uc in range(4):
                yps = psum[0:4]
                load_pads(uc)
                for ccl in range(4):
                    cc = uc * 4 + ccl
                    for d in range(2):
                        col = d * 16 + cc
                        off = 0 if d == 0 else CTX
                        lat0 = CTX - off
                        c1 = lambda tl: tl.v(S_[:, col:col + 1])
                        Hs, Ls = HIf, LOf
                        hb = lambda ri: View(Hs.t[:, ri, col, :].unsqueeze(2).to_broadcast([128, NS, NS]), Hs.bufs)
                        lb = lambda ri: View(Ls.t[:, ri, col, :].unsqueeze(1).to_broadcast([128, NS, NS]), Ls.bufs)
                        v3 = lambda tl: View(tl.t[:, 0:NS * NS].rearrange("p (a b) -> p a b", b=NS), tl.bufs)
                        kb.I("dve", "tensor_tensor", v3(t1), hb(1), lb(1), ALU.mult)
                        kb.I("dve", "tensor_tensor", v3(Rre), hb(0), lb(0), ALU.mult)
                        kb.I("dve", "tensor_tensor", v3(Rre), v3(Rre), v3(t1), ALU.subtract)
                        kb.I("pool", "tensor_tensor", v3(t2), hb(0), lb(1), ALU.mult)
                        kb.I("pool", "tensor_tensor", v3(Rim), hb(1), lb(0), ALU.mult)
                        kb.I("pool", "tensor_tensor", v3(Rim), v3(Rim), v3(t2), ALU.add)
                        if "r_dbg" in dbg and uc == 0 and ccl == 0:
                            kb.dma("sp", rdb.v(rdb.ap[d, 0]), Rre.v(), adds=True)
                            kb.dma("sp", rdb.v(rdb.ap[d, 1]), Rim.v(), adds=True)

                        def rv(tl, a0, a1):
                            return tl.v(S_[:, a0:a1])
                        for b0 in range(0, NT, 512):
                            bw = min(512, NT - b0)
                            pr, pi_ = psum[4 + 2 * (it % 2)], psum[5 + 2 * (it % 2)]
                            br_, bi_ = bre[it % 2], bim[it % 2]
                            it += 1
                            kb.I("pe", "matmul", pr.v(S_[:, 0:bw]), Bpad.v(S_[:, ccl, 0, :]), useq.v(S_[:, uc, off + b0:off + b0 + bw], uc), start=True, stop=True)
                            kb.I("pe", "matmul", pi_.v(S_[:, 0:bw]), Bpad.v(S_[:, ccl, 1, :]), useq.v(S_[:, uc, off + b0:off + b0 + bw], uc), start=True, stop=True)
                            kb.I("act", "activation", br_.v(S_[:, 0:bw]), pr.v(S_[:, 0:bw]), AF.Copy)
                            kb.I("act", "activation", bi_.v(S_[:, 0:bw]), pi_.v(S_[:, 0:bw]), AF.Copy)
                            sl = S_[:, b0:b0 + bw]
                            kb.I("dve", "tensor_tensor", t1.v(sl), rv(Rim, b0, b0 + bw), bi_.v(S_[:, 0:bw]), ALU.mult)
                            kb.I("dve", "tensor_tensor", Zre.v(sl), rv(Rre, b0, b0 + bw), br_.v(S_[:, 0:bw]), ALU.mult)
                            kb.I("dve", "tensor_tensor", Zre.v(sl), Zre.v(sl), t1.v(sl), ALU.subtract)
                            kb.I("pool", "tensor_tensor", t2.v(sl), rv(Rim, b0, b0 + bw), br_.v(S_[:, 0:bw]), ALU.mult)
                            kb.I("pool", "tensor_tensor", Zim.v(sl), rv(Rre, b0, b0 + bw), bi_.v(S_[:, 0:bw]), ALU.mult)
                            kb.I("pool", "tensor_tensor", Zim.v(sl), Zim.v(sl), t2.v(sl), ALU.add)
                        rb = View(rr.t[:, col:col + 1].to_broadcast([128, NT]), rr.bufs)
                        if d == 0:
                            kb.I("dve", "tensor_tensor_scan", Sre.v(), rb, Zre.v(), 0.0, ALU.mult, ALU.add)
                            kb.I("dve", "tensor_tensor_scan", Sim.v(), rb, Zim.v(), 0.0, ALU.mult, ALU.add)
                        else:
                            kb.I("dve", "tensor_tensor_scan", View(Sre.t[:, ::-1], Sre.bufs), rb, View(Zre.t[:, ::-1], Zre.bufs), 0.0, ALU.mult, ALU.add)
                            kb.I("dve", "tensor_tensor_scan", View(Sim.t[:, ::-1], Sim.bufs), rb, View(Zim.t[:, ::-1], Zim.bufs), 0.0, ALU.mult, ALU.add)
                        sl = S_[:, lat0:lat0 + SEQ]
                        kb.I("dve", "tensor_tensor", Zre.v(sl), Rre.v(sl), Sre.v(sl), ALU.mult)
                        kb.I("pool", "tensor_tensor", Zim.v(sl), Rre.v(sl), Sim.v(sl), ALU.mult)
                        kb.I("pool", "tensor_tensor", Rre.v(sl), Rim.v(sl), Sre.v(sl), ALU.mult)
                        kb.I("dve", "tensor_tensor", Rim.v(sl), Rim.v(sl), Sim.v(sl), ALU.mult)
                        kb.I("dve", "tensor_tensor", Zre.v(sl), Zre.v(sl), Rim.v(sl), ALU.add)
                        kb.I("pool", "tensor_tensor", Zim.v(sl), Zim.v(sl), Rre.v(sl), ALU.subtract)
                        ce_r, ce_i = CeR[d], CeI[d]
                        kb.I("dve", "tensor_scalar", ct.v(), Cpad.v(S_[:, ccl, 1, :]), c1(ci), None, ALU.mult)
                        kb.I("dve", "scalar_tensor_tensor", ce_r.v(), Cpad.v(S_[:, ccl, 0, :]), c1(cr), ct.v(), ALU.mult, ALU.subtract)
                        kb.I("dve", "tensor_scalar", ct.v(), Cpad.v(S_[:, ccl, 1, :]), c1(cr), None, ALU.mult)
                        kb.I("dve", "scalar_tensor_tensor", ce_i.v(), Cpad.v(S_[:, ccl, 0, :]), c1(nci), ct.v(), ALU.mult, ALU.subtract)
                        first = (ccl == 0 and d == 0)
                        last = (ccl == 3 and d == 1)
                        for blk in range(4):
                            s0 = lat0 + blk * 512
                            kb.I("pe", "matmul", yps[blk].v(), ce_r.v(), Zre.v(S_[:, s0:s0 + 512]), start=first, stop=False)
                            kb.I("pe", "matmul", yps[blk].v(), ce_i.v(), Zim.v(S_[:, s0:s0 + 512]), start=False, stop=last)
                for blk in range(4):
                    s0 = blk * 512
                    kb.I("dve", "scalar_tensor_tensor", ysb.v(S_[:, s0:s0 + 512]), useq.v(S_[:, uc, CTX + s0:CTX + s0 + 512], uc),
                         dcol.v(S_[:, uc:uc + 1]), yps[blk].v(), ALU.mult, ALU.add)
                L = S_[:, 0:SEQ]
                kb.I("dve", "tensor_tensor", gt1.v(L), ysb.v(L), ysb.v(L), ALU.mult)
                kb.I("dve", "tensor_scalar", gt1.v(L), gt1.v(L), 0.044715, 1.0, ALU.mult, ALU.add)
                kb.I("dve", "tensor_tensor", gt1.v(L), gt1.v(L), ysb.v(L), ALU.mult)
                kb.I("act", "activation", gt1.v(L), gt1.v(L), AF.Sigmoid, scale=float(2.0 * np.sqrt(2.0 / np.pi)))
                kb.I("dve", "tensor_tensor", View(ge.t[:, uc, :].rearrange("p (r c) -> p c r", c=64), [ge.bufs[uc]]),
                     View(ysb.t[:, 0:SEQ].rearrange("p (c r) -> p c r", r=32), ysb.bufs),
                     View(gt1.t[:, 0:SEQ].rearrange("p (c r) -> p c r", r=32), gt1.bufs), ALU.mult)
            if "s5_dbg" in dbg:
                gd = Dram(kb, "s5_dbg", [4, 128, SEQ], BF16, kind="ExternalOutput")
                kb.dma("sp", gd.v(gd.ap.rearrange("k p t -> p k t")), ge.v())
            sgl = [Tile(kb, f"sgl{i}", [128, 512], F32) for i in range(2)]
            yo = [Tile(kb, f"yo{i}", [128, 512], BF16) for i in range(2)]
            n = 0
            for blk in range(4):
                for oc in range(4):
                    pa, pb = psum[(2 * n) % 8], psum[(2 * n + 1) % 8]
                    for (pt, o) in ((pa, oc), (pb, oc + 4)):
                        for kc in range(4):
                            kb.I("pe", "matmul", pt.v(), gluw.v(S_[:, kc, o * 128:(o + 1) * 128]), ge.v(S_[:, kc, blk * 512:(blk + 1) * 512], kc),
                                 start=(kc == 0), stop=(kc == 3))
                    sg_ = sgl[n % 2]
                    kb.I("act", "activation", sg_.v(), pb.v(), AF.Sigmoid)
                    kb.I("dve", "tensor_tensor", yo[n % 2].v(), sg_.v(), pa.v(), ALU.mult)
                    kb.dma("sp", yT.v(yT.ap[12 + oc, :, blk * 512:(blk + 1) * 512], 12 + oc), yo[n % 2].v(), adds=True)
                    n += 1

    if STOP_AFTER >= 5:
        with kb.phase():
            wo = Tile(kb, "wo", [128, KC, D], BF16)
            for q4 in range(4):
                kb.dma("pool", wo.v(S_[:, q4 * 4:(q4 + 1) * 4, :]), woutP.v(woutP.ap[:, q4 * 4:(q4 + 1) * 4, :]), adds=True)
            yb = [Tile(kb, f"yb{i}", [128, KC, 512], BF16) for i in range(2)]
            xr_ = [Tile(kb, f"xrF{i}", [128, 512], F32) for i in range(2)]
            xo_ = [Tile(kb, f"xoF{i}", [128, 512], F32) for i in range(2)]
            for blk in range(4):
                t0 = blk * 512
                ybt = yb[blk % 2]
                kb.dma("sp", ybt.v(), yT.v(yT.ap[:, :, t0:t0 + 512].rearrange("k p t -> p k t")))
                for dc in range(KC):
                    po = psum[dc % 4]
                    xr = xr_[dc % 2]
                    kb.dma("sp", xr.v(), x1T.v(x1T.ap[dc, :, t0:t0 + 512]))
                    for ic in range(KC):
                        kb.I("pe", "matmul", po.v(), wo.v(S_[:, ic, dc * 128:(dc + 1) * 128]), ybt.v(S_[:, ic, :]),
                             start=(ic == 0), stop=(ic == KC - 1))
                    o = xo_[dc % 2]
                    kb.I("dve", "scalar_tensor_tensor", o.v(), po.v(), gate.v(S_[:, 1, dc, 0:1]), xr.v(), ALU.mult, ALU.add)
                    kb.dma("sp", x2T.v(x2T.ap[dc, :, t0:t0 + 512], blk), o.v(), adds=True)

    if STOP_AFTER >= 6:
        with kb.phase():
            ffn_phase(up2P, dn2P, x2T, x3T, 2, blocks[:4])

    if STOP_AFTER >= 7:
        with kb.phase():
            nb = NormBufs()
            fo = Tile(kb, "fo", [128, KC, TB], F32, nbuf=KC)
            for bi, (t0, tb, t) in enumerate(blocks[:4]):
                norm_mod(nb, x3T, t0, tb, 0, 0, lambda kc, a, b: fo.v(S_[:, kc, a:b], kc),
                         G=lambda kc: nrm.v(S_[:, 3, kc:kc + 1]), Sv=False)
                kb.dma("sp", outT.v(outT.ap[:, :, t0:t0 + tb].rearrange("k p t -> p k t")), fo.v(S_[:, :, 0:tb]), adds=True)

    kb.finish(x1T.bufs + outT.bufs + h2T.bufs + uS.bufs + yT.bufs + x2T.bufs + x3T.bufs)
    return nc, kb


STOP_AFTER = 99
NBLK = 99
OVERLAP = True
NHEADS = NH
DBG_H = 0


def host_layout(inputs, b):
    f = np.float32
    x = inputs["x"][b]
    ctx = inputs["ctx"][b]
    xcat = np.concatenate([x, ctx], axis=0)
    xT = np.ascontiguousarray(xcat.T).reshape(KC, 128, NT)
    cc = np.stack([inputs["c"][b], inputs["c_ctx"]], axis=1)
    cT = np.ascontiguousarray(cc.reshape(KC, 128, 2).transpose(1, 0, 2))
    return {"xT": xT.astype(f), "cT": cT.astype(f)}


_SHARED = {}


def host_shared(inputs):
    f = np.float32
    sh = {}
    wmod = inputs["w_mod"][0]
    sh["wmodP"] = np.ascontiguousarray(wmod.reshape(KC, 128, 36, 512).transpose(2, 1, 0, 3))
    sh["bmodT"] = np.ascontiguousarray(inputs["b_mod"][0].reshape(NMOD * KC, 128).T)
    norms = np.stack([inputs["norm_ffn1"][0], inputs["norm_mix"][0], inputs["norm_ffn2"][0], inputs["final_norm"]], 0)
    sh["normsT"] = np.ascontiguousarray(norms.reshape(4, KC, 128).transpose(2, 0, 1))
    for nm, k in (("up1P", "ffn1_up"), ("up2P", "ffn2_up")):
        w = inputs[k][0]
        g = w[:, :DFF].reshape(KC, 128, NJ, 128)
        u = w[:, DFF:].reshape(KC, 128, NJ, 128)
        gu = np.concatenate([g, u], axis=3)
        sh[nm] = np.ascontiguousarray(gu.transpose(2, 1, 0, 3))
    for nm, k in (("dn1P", "ffn1_down"), ("dn2P", "ffn2_down")):
        w = inputs[k][0]
        sh[nm] = np.ascontiguousarray(w.reshape(NJ, 128, KC, 128).transpose(2, 1, 0, 3))
    win = inputs["w_in"][0]
    wsm = np.concatenate([win[:, 6144:6192], win[:, 6192:6704]], axis=1)
    sh["winS"] = np.ascontiguousarray(wsm.reshape(KC, 128, 560).transpose(1, 0, 2))
    sh["dnab"] = np.ascontiguousarray(np.stack([inputs["dn_a_log"][0].reshape(24), inputs["dn_dt_bias"][0].reshape(24)], 1))
    blocks_ = []
    for h in range(NH):
        cols = [win[:, c * 1536 + h * 128:c * 1536 + (h + 1) * 128] for c in range(4)]
        blocks_.append(np.concatenate(cols, axis=1).reshape(KC, 128, 512).transpose(1, 0, 2))
    sh["winP"] = np.ascontiguousarray(np.stack(blocks_, 0))
    cw = inputs["dn_conv"][0].reshape(9, 36, 128)
    sh["convT"] = np.ascontiguousarray(cw.transpose(2, 1, 0))
    sh["dnnorm"] = np.ascontiguousarray(inputs["dn_norm"][0].reshape(128, 1))
    def chan(a):
        return a.reshape(2, 16, 2, 64).transpose(2, 3, 0, 1).reshape(128, 32)
    ldt = np.broadcast_to(inputs["s5_log_dt"][0][:, :, None], (2, 32, 64))
    sh["s5a"] = np.ascontiguousarray(np.stack([chan(inputs["s5_a_re"][0]), chan(inputs["s5_a_im"][0]), chan(ldt)], axis=1))
    sh["s5bT"] = np.ascontiguousarray(np.stack([inputs["s5_b_re"][0].transpose(0, 2, 1), inputs["s5_b_im"][0].transpose(0, 2, 1)], 0))
    sh["s5cT"] = np.ascontiguousarray(np.stack([inputs["s5_c_re"][0].transpose(0, 2, 1), inputs["s5_c_im"][0].transpose(0, 2, 1)], 0))
    sh["s5dT"] = np.ascontiguousarray(inputs["s5_d"][0].reshape(4, 128).T)
    sh["gluP"] = np.ascontiguousarray(inputs["s5_glu"][0].reshape(4, 128, 1024).transpose(1, 0, 2))
    sh["woutP"] = np.ascontiguousarray(inputs["w_out"][0].reshape(KC, 128, D).transpose(1, 0, 2))
    i = np.arange(128)
    ident = np.eye(128, dtype=f)
    Lincl = (i[:, None] <= i[None, :]).astype(f)
    Uincl = (i[:, None] >= i[None, :]).astype(f)
    Lstr = (i[:, None] > i[None, :]).astype(f)
    Ustr = (i[:, None] < i[None, :]).astype(f)
    ones = np.ones((128, 128), f)
    bd = ((i[:, None] // 32) == (i[None, :] // 32)).astype(f)
    sh["consts"] = np.ascontiguousarray(np.stack([ident, ones, Lincl, Uincl, Lstr, Ustr, bd, 1.0 - bd], axis=1))
    return {k: v.astype(f) for k, v in sh.items()}


def kernel(**inputs):
    inputs = {k: np.asarray(v) for k, v in inputs.items()}
    nc, kb = build_program()
    sh = host_shared(inputs)
    in_maps = []
    for b in range(8):
        mp = dict(sh)
        mp.update(host_layout(inputs, b))
        in_maps.append(mp)
    res = run_bass_kernel_spmd(nc, in_maps, core_ids=list(range(8)))
    out = np.stack([r["outT"].reshape(D, SEQ).T for r in res.results], axis=0)
    return np.ascontiguousarray(out.astype(np.float32))
```
